# Optimizing a Trainium2 kernel written in Bass

```python
import math
import jax, jax.numpy as jnp
from jax import lax
import numpy as np

D_MODEL = 2048
BATCH = 32
SEQ = 256
DEPTH = 4
DEC_BATCH = 4
DEC_SEQ = 2048
PAST_LEN = 256

GRID_W = 64
MIX_W = D_MODEL
H_M = 4
DH_M = MIX_W // 2 // H_M
H_R = 4
DH_R = MIX_W // 2 // H_R
W_M = H_M * DH_M
W_R = H_R * DH_R
N_GATES = 4
D_IN = 4 * W_M + 4 * W_R + N_GATES * H_M
D_FF = -(-8 * D_MODEL // (3 * 256)) * 256
CHUNK = 128
ROPE_BASE = 10000.0
EPS = 1e-6

kernel_name = "hybrid_mlstm_retention_diffusion_step"

F32 = jnp.float32


def rms_norm(x, g):
    xf = x.astype(F32)
    y = xf * lax.rsqrt(jnp.mean(xf * xf, axis=-1, keepdims=True) + EPS)
    return (y * g.astype(F32)).astype(x.dtype)


def head_rms_norm(h, g):
    H, d = h.shape[1], h.shape[3]
    y = h * lax.rsqrt(jnp.mean(h * h, axis=-1, keepdims=True) + EPS)
    return y * g.astype(F32).reshape(1, H, 1, d)


def axial_rope(x):
    L, d = x.shape[2], x.shape[3]
    rows = L // GRID_W
    r = jnp.repeat(jnp.arange(rows), GRID_W)
    col = jnp.tile(jnp.arange(GRID_W), rows)
    half = d // 2
    quarter = half // 2
    inv = ROPE_BASE ** (-jnp.arange(quarter, dtype=F32) / quarter)

    def rot(xh, pos):
        ang = pos.astype(F32)[:, None] * inv[None, :]
        cos, sin = jnp.cos(ang), jnp.sin(ang)
        x1, x2 = xh[..., :quarter], xh[..., quarter:]
        return jnp.concatenate([x1 * cos - x2 * sin, x1 * sin + x2 * cos], axis=-1)

    return jnp.concatenate([rot(x[..., :half], r), rot(x[..., half:], col)], axis=-1)


def mlstm_chunked(q, k, v, ig, lf, C0, n0, m0):
    B, H, L, _ = q.shape
    N = L // CHUNK

    def chunks(z):
        return jnp.moveaxis(z.reshape(B, H, N, CHUNK, *z.shape[3:]), 2, 0)

    mask = jnp.tril(jnp.ones((CHUNK, CHUNK), dtype=bool))

    def step(carry, xs):
        Cm, nv, m = carry
        qc, kc, vc, ic, fc = xs
        b = jnp.cumsum(fc, axis=-1)
        log_inter = b + m[..., None]
        log_intra = jnp.where(mask, b[..., :, None] - b[..., None, :] + ic[..., None, :], -jnp.inf)
        m_t = jnp.maximum(log_inter, jnp.max(log_intra, axis=-1))
        a = jnp.exp(log_inter - m_t)
        w = jnp.exp(log_intra - m_t[..., None])
        s = jnp.einsum('bhtd,bhsd->bhts', qc, kc) * w
        num = a[..., None] * jnp.einsum('bhtd,bhde->bhte', qc, Cm) + jnp.einsum('bhts,bhse->bhte', s, vc)
        den = a * jnp.einsum('bhtd,bhd->bht', qc, nv) + jnp.sum(s, axis=-1)
        h = num / jnp.maximum(jnp.abs(den), jnp.exp(-m_t))[..., None]
        b_last = b[..., -1]
        log_w = b_last[..., None] - b + ic
        m_new = jnp.maximum(b_last + m, jnp.max(log_w, axis=-1))
        carry_scale = jnp.exp(b_last + m - m_new)
        wk = kc * jnp.exp(log_w - m_new[..., None])[..., None]
        C_new = carry_scale[..., None, None] * Cm + jnp.einsum('bhsd,bhse->bhde', wk, vc)
        n_new = carry_scale[..., None] * nv + jnp.sum(wk, axis=2)
        return (C_new, n_new, m_new), h

    (C, n, m), hs = lax.scan(step, (C0, n0, m0),
                             (chunks(q), chunks(k), chunks(v), chunks(ig), chunks(lf)))
    h = jnp.moveaxis(hs, 0, 2).reshape(B, H, L, v.shape[-1])
    return h, C, n, m


def retention_chunked(q, k, v, lg, S0):
    B, H, L, _ = q.shape
    dv = v.shape[-1]
    N = L // CHUNK
    q, k, v = [z.reshape(B, H, N, CHUNK, -1) for z in (q, k, v)]
    idx = jnp.arange(CHUNK, dtype=F32)
    diff = idx[:, None] - idx[None, :]
    D = jnp.where(diff >= 0, jnp.exp(jnp.maximum(diff, 0.0)[None] * lg[:, None, None]), 0.0)
    s = jnp.einsum('bhntd,bhnsd->bhnts', q, k) * D[None, :, None]
    intra = jnp.einsum('bhnts,bhnse->bhnte', s, v)
    kdec = k * jnp.exp((CHUNK - 1 - idx)[None, :] * lg[:, None])[None, :, None, :, None]
    kv = jnp.einsum('bhnsd,bhnse->bhnde', kdec, v)
    cdec = jnp.exp(CHUNK * lg)[None, :, None, None]

    def step(S, kv_n):
        return cdec * S + kv_n, S

    S_fin, S_starts = lax.scan(step, S0, jnp.moveaxis(kv, 2, 0))
    S_starts = jnp.moveaxis(S_starts, 0, 2)
    qdec = q * jnp.exp((idx + 1)[None, :] * lg[:, None])[None, :, None, :, None]
    o = intra + jnp.einsum('bhntd,bhnde->bhnte', qdec, S_starts)
    return o.reshape(B, H, L, dv), S_fin


def token_mixer(h, st, w_in, gate_b, ret_decay, mnorm_g, rnorm_g, w_out, latent):
    B, L, _ = h.shape
    proj = (h @ w_in).astype(F32)
    cuts = [W_M, 2 * W_M, 3 * W_M, 4 * W_M, 4 * W_M + W_R, 4 * W_M + 2 * W_R,
            4 * W_M + 3 * W_R, 4 * W_M + 4 * W_R]
    mq, mk, mv, mo, rq, rk, rv, rg, mg = jnp.split(proj, cuts, axis=-1)

    def heads(z, H):
        return z.reshape(B, L, H, -1).transpose(0, 2, 1, 3)

    def merge(z):
        return z.transpose(0, 2, 1, 3).reshape(B, L, -1)

    def flip(z):
        return jnp.flip(z, axis=2)

    mC, mn, mm, rS = [s.astype(F32) for s in st]

    mq = heads(mq, H_M)
    mk = heads(mk, H_M) * (DH_M ** -0.5)
    mv = heads(mv, H_M)
    gates = (mg.reshape(B, L, N_GATES, H_M) + gate_b.astype(F32)).transpose(0, 2, 3, 1)
    i_f, f_f = gates[:, 0], jax.nn.log_sigmoid(gates[:, 1])
    i_b, f_b = gates[:, 2], jax.nn.log_sigmoid(gates[:, 3])
    hf, Cf, nf, mf = mlstm_chunked(mq, mk, mv, i_f, f_f, mC[:, 0], mn[:, 0], mm[:, 0])
    hb, Cb, nb, mb = mlstm_chunked(flip(mq), flip(mk), flip(mv), flip(i_b), flip(f_b),
                                   mC[:, 1], mn[:, 1], mm[:, 1])
    hm = head_rms_norm(hf + flip(hb), mnorm_g) * jax.nn.sigmoid(heads(mo, H_M))

    rq = heads(rq, H_R) * (DH_R ** -0.5)
    rk = heads(rk, H_R)
    rv = heads(rv, H_R)
    if latent:
        rq = axial_rope(rq)
        rk = axial_rope(rk)
    lg = -jnp.exp(ret_decay.astype(F32))
    of, Sf = retention_chunked(rq, rk, rv, lg[0], rS[:, 0])
    ob, Sb = retention_chunked(flip(rq), flip(rk), flip(rv), lg[1], rS[:, 1])
    hr = head_rms_norm(of + flip(ob), rnorm_g) * jax.nn.silu(heads(rg, H_R))

    y = jnp.concatenate([merge(hm), merge(hr)], axis=-1).astype(h.dtype)
    out = y @ w_out
    new_st = (jnp.stack([Cf, Cb], axis=1), jnp.stack([nf, nb], axis=1),
              jnp.stack([mf, mb], axis=1), jnp.stack([Sf, Sb], axis=1))
    return out, new_st


def trunk_layer(x, mod, st, g1, w_in, gate_b, ret_decay, mnorm_g, rnorm_g, w_out, g2, w_gu, w_down, latent):
    shift1, scale1, gate1, shift2, scale2, gate2 = jnp.split(mod[:, None, :], 6, axis=-1)
    h = rms_norm(x, g1) * (1 + scale1) + shift1
    out, st_new = token_mixer(h, st, w_in, gate_b, ret_decay, mnorm_g, rnorm_g, w_out, latent)
    x = x + gate1 * out
    h = rms_norm(x, g2) * (1 + scale2) + shift2
    a, u = jnp.split(h @ w_gu, 2, axis=-1)
    x = x + gate2 * ((jax.nn.silu(a) * u) @ w_down)
    return x, st_new


def setup_inputs(seed: int = 0) -> dict:
    key = jax.random.key(seed)
    ks = jax.random.split(key, 24)
    nrm = jax.random.normal
    d = D_MODEL
    f_bias = jnp.linspace(3.0, 6.0, H_M, dtype=F32)
    z = jnp.zeros((H_M,), F32)
    gate_base = jnp.stack([z, f_bias, z, f_bias])
    dec_base = jnp.log(-jnp.log1p(-(2.0 ** (-5.0 - jnp.arange(H_R, dtype=F32)))))
    return {
        "x_prompt": nrm(ks[0], (BATCH, SEQ, d), F32),
        "x_sample": nrm(ks[1], (DEC_BATCH, DEC_SEQ, d), F32),
        "c": nrm(ks[2], (DEC_BATCH, d), F32),
        "state_mlstm_C": nrm(ks[3], (DEC_BATCH, DEPTH, 2, H_M, DH_M, DH_M), F32),
        "state_mlstm_n": nrm(ks[4], (DEC_BATCH, DEPTH, 2, H_M, DH_M), F32),
        "state_mlstm_m": nrm(ks[5], (DEC_BATCH, DEPTH, 2, H_M), F32),
        "state_ret_S": nrm(ks[6], (DEC_BATCH, DEPTH, 2, H_R, DH_R, DH_R), F32),
        "c_ctx": nrm(ks[7], (d,), F32),
        "w_mod": nrm(ks[8], (DEPTH, d, 6 * d), F32) * (0.5 * d ** -0.5),
        "b_mod": nrm(ks[9], (DEPTH, 6 * d), F32) * 0.02,
        "norm1_g": 1.0 + 0.05 * nrm(ks[10], (DEPTH, d), F32),
        "w_in": nrm(ks[11], (DEPTH, d, D_IN), F32) * d ** -0.5,
        "mlstm_gate_b": gate_base[None] + 0.1 * nrm(ks[12], (DEPTH, N_GATES, H_M), F32),
        "ret_decay": dec_base[None, None] + 0.05 * nrm(ks[13], (DEPTH, 2, H_R), F32),
        "mlstm_norm_g": 1.0 + 0.05 * nrm(ks[14], (DEPTH, W_M), F32),
        "ret_norm_g": 1.0 + 0.05 * nrm(ks[15], (DEPTH, W_R), F32),
        "w_out": nrm(ks[16], (DEPTH, MIX_W, d), F32) * MIX_W ** -0.5,
        "norm2_g": 1.0 + 0.05 * nrm(ks[17], (DEPTH, d), F32),
        "w_gu": nrm(ks[18], (DEPTH, d, 2 * D_FF), F32) * d ** -0.5,
        "w_down": nrm(ks[19], (DEPTH, D_FF, d), F32) * D_FF ** -0.5,
        "final_norm_g": 1.0 + 0.05 * nrm(ks[20], (d,), F32),
    }


def reference(x_prompt, x_sample, c, state_mlstm_C, state_mlstm_n, state_mlstm_m, state_ret_S,
              c_ctx, w_mod, b_mod, norm1_g, w_in, mlstm_gate_b, ret_decay, mlstm_norm_g, ret_norm_g,
              w_out, norm2_g, w_gu, w_down, final_norm_g):
    Bp = x_prompt.shape[0]
    zero_st = (jnp.zeros((Bp, 2, H_M, DH_M, DH_M), F32), jnp.zeros((Bp, 2, H_M, DH_M), F32),
               jnp.zeros((Bp, 2, H_M), F32), jnp.zeros((Bp, 2, H_R, DH_R, DH_R), F32))
    xp, xs = x_prompt, x_sample
    ctx_C, ctx_n, ctx_m, ctx_S = [], [], [], []
    for l in range(DEPTH):
        mod_ctx = (jax.nn.silu(c_ctx) @ w_mod[l] + b_mod[l])[None]
        mod_lat = jax.nn.silu(c) @ w_mod[l] + b_mod[l]
        lw = (norm1_g[l], w_in[l], mlstm_gate_b[l], ret_decay[l], mlstm_norm_g[l], ret_norm_g[l],
              w_out[l], norm2_g[l], w_gu[l], w_down[l])
        xp, st = trunk_layer(xp, mod_ctx, zero_st, *lw, latent=False)
        ctx_C.append(st[0]); ctx_n.append(st[1]); ctx_m.append(st[2]); ctx_S.append(st[3])
        cache_st = (state_mlstm_C[:, l], state_mlstm_n[:, l], state_mlstm_m[:, l], state_ret_S[:, l])
        xs, _ = trunk_layer(xs, mod_lat, cache_st, *lw, latent=True)
    y_prompt = rms_norm(xp, final_norm_g)
    y_sample = rms_norm(xs, final_norm_g)
    dt = x_prompt.dtype
    new_mlstm_C = jnp.stack(ctx_C, axis=1).astype(dt)
    new_mlstm_n = jnp.stack(ctx_n, axis=1).astype(dt)
    new_mlstm_m = jnp.stack(ctx_m, axis=1).astype(dt)
    new_ret_S = jnp.stack(ctx_S, axis=1).astype(dt)
    return (y_prompt, y_sample, new_mlstm_C, new_mlstm_n, new_mlstm_m, new_ret_S)
```

```python
import numpy as np
import ml_dtypes
import concourse.bass as bass
import concourse.mybir as mybir
from concourse.bass_utils import run_bass_kernel_spmd

F32 = mybir.dt.float32
BF16 = mybir.dt.bfloat16
AF = mybir.ActivationFunctionType
ALU = mybir.AluOpType
AX = mybir.AxisListType

D = 2048
T = 2048
NCH = 16
DFF = 5632
DIN = 8208
EPS = 1e-6
NL = 4
PH = "*PH*"


class Op:
    __slots__ = ("eng", "fn", "deps", "dma_key", "needs_inc", "inc_val", "scope")


class Prog:
    ENGS = ("pe", "act", "dve", "pool", "sp")

    def __init__(self):
        self.ops = {e: [] for e in self.ENGS}
        self.last_w = {}
        self.readers = {}
        self.dma_cnt = {}
        self.dry = False
        self.scope = None
        self.use_scopes = False

    def add(self, eng, fn, r=(), w=(), dma=None, phase=True):
        if self.dry:
            return None
        op = Op()
        op.eng = eng
        op.fn = fn
        op.dma_key = dma
        op.needs_inc = False
        op.inc_val = 0
        op.scope = self.scope
        deps = []
        r = list(r)
        w = list(w)
        if phase:
            r.append(PH)

        def dep(o):
            if o is None:
                return
            if o.dma_key is not None:
                deps.append((o, self.dma_cnt[o.dma_key]))
            else:
                if o.eng == "pe" and eng == "pe" and dma is None:
                    return
                o.needs_inc = True
                deps.append((o, None))

        for k in r:
            dep(self.last_w.get(k))
        for k in w:
            dep(self.last_w.get(k))
            for o in self.readers.get(k, ()):
                dep(o)
        if dma is not None:
            self.dma_cnt[dma] = self.dma_cnt.get(dma, 0) + 16
        for k in r:
            self.readers.setdefault(k, []).append(op)
        for k in w:
            self.last_w[k] = op
            self.readers[k] = []
        op.deps = deps
        self.ops[eng].append(op)
        return op

    def emit(self, nc):
        for e, ops in self.ops.items():
            n = 0
            for op in ops:
                if op.dma_key is None and op.needs_inc:
                    n += 1
                    op.inc_val = n
        sems = {}
        for e in self.ENGS:
            sems[("c", e)] = nc.alloc_semaphore("c_" + e)
        for i, k in enumerate(self.dma_cnt):
            sems[("d", k)] = nc.alloc_semaphore("d_%d" % i)
        prog = self

        def run(name, eng):
            seen = {}
            cur_scope = None
            for op in prog.ops[name]:
                if prog.use_scopes and op.scope != cur_scope:
                    if cur_scope is not None:
                        nc.pop_named_scope(cur_scope)
                    if op.scope is not None:
                        nc.push_named_scope(op.scope)
                    cur_scope = op.scope
                need = {}
                for (o, v) in op.deps:
                    if o.dma_key is not None:
                        key = ("d", o.dma_key)
                        val = v
                    else:
                        key = ("c", o.eng)
                        val = o.inc_val
                    if val > need.get(key, 0):
                        need[key] = val
                for key, val in need.items():
                    if seen.get(key, 0) >= val:
                        continue
                    eng.wait_ge(sems[key], val)
                    seen[key] = val
                ins = op.fn(eng)
                if op.dma_key is not None:
                    ins.then_inc(sems[("d", op.dma_key)], 16)
                elif op.needs_inc:
                    ins.then_inc(sems[("c", name)], 1)
            if prog.use_scopes and cur_scope is not None:
                nc.pop_named_scope(cur_scope)
            if name == "sp":
                for k, tot in prog.dma_cnt.items():
                    if seen.get(("d", k), 0) < tot:
                        eng.wait_ge(sems[("d", k)], tot)

        with nc.Block() as block:
            @block.tensor
            def _(e):
                run("pe", e)

            @block.scalar
            def _(e):
                run("act", e)

            @block.vector
            def _(e):
                run("dve", e)

            @block.gpsimd
            def _(e):
                run("pool", e)

            @block.sync
            def _(e):
                run("sp", e)


class Arena:
    def __init__(self, ar, nwords):
        self.ar = ar
        self.nbytes = nwords * 4

    def view(self, off, nelem, dtype, parts=128):
        assert off % 4 == 0
        nb = nelem * (4 if dtype == F32 else 2)
        nb4 = (nb + 3) // 4 * 4
        assert off + nb4 <= self.nbytes, (off, nb4, self.nbytes)
        ap = self.ar[0:parts, off // 4:(off + nb4) // 4]
        if dtype != F32:
            ap = ap.bitcast(dtype)
            if nb4 != nb:
                ap = ap[:, 0:nelem]
        return ap


def build(nl=NL, scopes=False):
    nc = bass.Bass("TRN2", target_bir_lowering=False)
    P = Prog()
    P.use_scopes = scopes

    def din(name, shape, dtype=F32):
        return nc.dram_tensor(name, shape, dtype, kind="ExternalInput").ap()

    def dout(name, shape, dtype=F32):
        return nc.dram_tensor(name, shape, dtype, kind="ExternalOutput").ap()

    x_in = din("x_in", [16, 128, T])
    vec_fm = din("vec_fm", [128, 544])
    gb_in = din("gb", [4, 32])
    m0_in = din("m0", [4, 8])
    rdec_in = din("rdec", [1, 32])
    kap_in = din("kap", [1, 32])
    initC = din("init_Caug", [NL, 2, 4, 256, 257])
    initS = din("init_S", [NL, 2, 4, 256, 256])
    w_mod = din("w_mod", [NL, D, 6 * D])
    w_in = din("w_in", [NL, D, DIN])
    w_out = din("w_out", [NL, D, D])
    w_gu = din("w_gu", [NL, D, 2 * DFF])
    w_down = din("w_down", [NL, DFF, D])
    mng = din("mnorm_g", [NL, 1024])
    rng_ = din("rnorm_g", [NL, 1024])
    cst_f = din("cst_f", [128, 1536])
    cst_b = din("cst_b", [128, 384], BF16)
    keep_in = din("keep", [4, T])
    rope_in = din("rope", [128, 2 * 16 * 64 + 2 * 64])

    yT_out = dout("yT", [16, 128, T])
    oC = dout("o_Caug", [8, NL, 2, 4, 256, 257])
    oS = dout("o_S", [8, NL, 2, 4, 256, 256])
    oM = dout("o_m", [4, NL, 2, 8])
    xT = nc.dram_tensor("xT_scr", [16, 128, T], F32, kind="Internal").ap()

    NW = 52900
    AR = nc.alloc_sbuf_tensor("arena", [128, NW], F32)
    A = Arena(AR, NW)
    off = [0]

    def alloc(nelem, dtype, parts=128):
        nb = (nelem * (4 if dtype == F32 else 2) + 31) // 32 * 32
        o = off[0]
        off[0] += nb
        return A.view(o, nelem, dtype, parts), o

    ident_b, _ = alloc(128, BF16)
    mask_b2 = [alloc(128, BF16)[0], alloc(128, BF16)[0]]
    ident_f, _ = alloc(128, F32)
    ones_f, _ = alloc(128, F32)
    vecs, _ = alloc(544, F32)
    modv, _ = alloc(4 * 96, F32)
    modA, _ = alloc(4 * 96, F32)
    sT, _ = alloc(16, BF16)
    csil, _ = alloc(16, F32)
    kapbc, _ = alloc(32, F32)
    gb4, _ = alloc(32, F32)
    m04, _ = alloc(8, F32)
    lgb, _ = alloc(8, F32)
    rs_row, _ = alloc(8, F32)
    rs_wsc, _ = alloc(8, F32)
    rs_wst, _ = alloc(8, F32)
    rs_a128, _ = alloc(8, F32)
    rs_a128k, _ = alloc(128, F32)
    jtab, _ = alloc(4, F32)
    gtok, _ = alloc(256, F32)
    gbc, _ = alloc(128, F32)
    g_umax, _ = alloc(16, F32)
    g_M, _ = alloc(16, F32)
    g_mk, _ = alloc(16, F32)
    g_m, _ = alloc(16, F32)
    g_Bl, _ = alloc(16, F32)
    g_ast, _ = alloc(16, F32)
    g_a4, _ = alloc(64, F32)
    sel, _ = alloc(64, F32)
    rope_t, _ = alloc(2 * 1024 + 128, F32)
    Gn, _ = alloc(256, F32)
    small, _ = alloc(16, F32)
    epsT, _ = alloc(1, F32)
    wslot = [alloc(16 * 256, BF16)[0], alloc(16 * 256, BF16)[0]]
    BIG = off[0]
    o_hT = BIG
    o_yT = o_hT + 65536
    o_head = o_yT + 32768
    o_qT = o_head
    o_kT = o_qT + 8192
    o_ktok = o_kT + 8192
    o_vaug = o_ktok + 8192
    o_og = o_vaug + 8256
    o_hsum = o_og + 8192
    o_scan = o_hsum + 16384
    o_Cst = o_scan
    o_Cbf = o_Cst + 8224
    o_STb = o_Cbf + 2064
    o_wk = o_STb + 512
    o_ytok = o_wk + 1024
    o_t1 = o_ytok + 1024
    o_end = o_t1 + 2048
    assert o_end <= NW * 4, (o_end, NW * 4)

    hT = A.view(o_hT, 16 * T, BF16).rearrange("p (c t) -> p c t", t=T)
    yT = A.view(o_yT, 8 * T, BF16).rearrange("p (c t) -> p c t", t=T)
    qT = A.view(o_qT, 2 * T, BF16).rearrange("p (c t) -> p c t", t=T)
    kT = A.view(o_kT, 2 * T, BF16).rearrange("p (c t) -> p c t", t=T)
    ktok = A.view(o_ktok, 16 * 256, BF16).rearrange("p (c e) -> p c e", e=256)
    vaug = A.view(o_vaug, 16 * 258, BF16).rearrange("p (c e) -> p c e", e=258)
    og = A.view(o_og, 16 * 256, BF16).rearrange("p (c e) -> p c e", e=256)
    hsum = A.view(o_hsum, 16 * 256, F32).rearrange("p (c e) -> p c e", e=256)
    ropeA = A.view(o_og, 1024, F32).rearrange("p (c e) -> p c e", e=64)
    ropeB = A.view(o_og + 4096, 1024, F32).rearrange("p (c e) -> p c e", e=64)
    Cst = [[A.view(o_Cst + (d * 2 + pp) * 2056, 514, F32).rearrange("p (c e) -> p c e", e=257)
            for pp in range(2)] for d in range(2)]
    Cbf = [A.view(o_Cbf + d * 1032, 516, BF16).rearrange("p (c e) -> p c e", e=258) for d in range(2)]
    STb = [A.view(o_STb + d * 256, 128, BF16) for d in range(2)]
    wkb = [A.view(o_wk + d * 512, 256, BF16) for d in range(2)]
    ytok = [A.view(o_ytok + i * 512, 256, BF16) for i in range(2)]
    t1 = [A.view(o_t1 + i * 1024, 256, F32) for i in range(2)]
    xtile = A.view(o_head, 16 * 512, F32).rearrange("p (c t) -> p c t", t=512)
    sqt = [A.view(o_head + 32768 + i * 2048, 512, F32) for i in range(2)]
    rbt = A.view(o_head + 32768 + 4096, 512, F32)
    tmpn = [A.view(o_head + 32768 + 6144 + i * 2048, 512, F32) for i in range(2)]
    GI = A.view(o_head, T, F32, parts=4)
    GE = A.view(o_head + 8192, T, F32, parts=4)
    GB = A.view(o_head + 16384, T, F32, parts=4)
    keepT = A.view(o_head + 24576, T, F32, parts=4)
    NXO = 16
    xp_o = [A.view(o_head + i * 2048, 512, F32) for i in range(NXO)]
    h2T = hT
    actT = A.view(o_yT, 22 * T, BF16).rearrange("p (c t) -> p c t", t=T)
    o_ffx = o_yT + 22 * T * 2
    NXF = 5
    xp_f = [A.view(o_ffx + i * 2048, 512, F32) for i in range(NXF)]
    sat = [A.view(o_ffx + NXF * 2048 + i * 2048, 512, F32) for i in range(2)]
    assert o_ffx + NXF * 2048 + 4096 <= NW * 4

    PS = [nc.alloc_psum_tensor("ps%d" % i, [128, 512], F32) for i in range(8)]

    def psk(b):
        return ("ps", b)

    def barrier():
        P.add("dve", lambda e: e.memset(small[:, 15:16], 0.0), w=[PH, "small15"], phase=False)

    def dma(q, out, in_, r, w, key, phase=True, **kw):
        return P.add(q, lambda e: e.dma_start(out=out, in_=in_, **kw), r=r, w=w, dma=key, phase=phase)

    def mm(out, lhsT, rhs, start, stop, r, w):
        P.add("pe", lambda e: e.matmul(out, lhsT, rhs, start=start, stop=stop), r=r, w=w)

    def act(out, in_, func, r, w, bias=None, scale=None, accum_out=None):
        kw = {}
        if bias is not None:
            kw["bias"] = bias
        if scale is not None:
            kw["scale"] = scale
        if accum_out is not None:
            kw["accum_out"] = accum_out
        P.add("act", lambda e: e.activation(out=out, in_=in_, func=func, **kw), r=r, w=w)

    def tt(out, in0, in1, op, r, w, eng="dve"):
        P.add(eng, lambda e: e.tensor_tensor(out=out, in0=in0, in1=in1, op=op), r=r, w=w)

    def ts(out, in0, s1, s2, op0, op1, r, w, eng="dve"):
        if s2 is None:
            P.add(eng, lambda e: e.tensor_scalar(out=out, in0=in0, scalar1=s1, scalar2=None, op0=op0), r=r, w=w)
        else:
            P.add(eng, lambda e: e.tensor_scalar(out=out, in0=in0, scalar1=s1, scalar2=s2, op0=op0, op1=op1), r=r, w=w)

    def stt(out, in0, scalar, in1, op0, op1, r, w):
        P.add("dve", lambda e: e.scalar_tensor_tensor(out=out, in0=in0, scalar=scalar, in1=in1, op0=op0, op1=op1), r=r, w=w)

    def cp(eng, out, in_, r, w):
        if eng == "act":
            P.add("act", lambda e: e.copy(out=out, in_=in_), r=r, w=w)
        else:
            P.add(eng, lambda e: e.tensor_copy(out=out, in_=in_), r=r, w=w)

    class WS:
        def __init__(self):
            self.descs = []
            self.n = 0
            self.issued = 0

        def _issue(self, i):
            parts = self.descs[i]
            s = i % 2
            for (dst_fn, src) in parts:
                dst = dst_fn(wslot[s])
                dma("pool", dst, src, r=[], w=[("w", s)], key=("w", s), phase=False)
            self.issued = i + 1

        def get(self, parts):
            i = self.n
            self.n += 1
            if P.dry:
                self.descs.append(parts)
                return wslot[i % 2], ("w", i % 2)
            while self.issued <= min(i + 1, len(self.descs) - 1):
                self._issue(self.issued)
            return wslot[i % 2], ("w", i % 2)

    ws = WS()

    def wtile_kc(src3, kcn, ncols):
        return [(lambda sl: sl[:, 0:kcn * ncols].rearrange("p (k n) -> p k n", n=ncols), src3)]

    def wview(slot, kcn, ncols):
        return slot[:, 0:kcn * ncols].rearrange("p (k n) -> p k n", n=ncols)

    def xsrc(l, first):
        return x_in if first else xT

    def program():
        ws.n = 0
        cload = []
        cload.append(dma("sp", ident_f[:, :], cst_f[:, 0:128], [], ["ident_f"], "c0"))
        dma("sp", ones_f[:, :], cst_f[:, 128:256], [], ["ones_f"], "c0")
        dma("sp", sel[0:4, :], cst_f[0:4, 256:320], [], ["sel"], "c0")
        dma("sp", jtab[:, :], cst_f[:, 320:324], [], ["jtab"], "c0")
        dma("sp", ident_b[:, :], cst_b[:, 0:128], [], ["ident_b"], "c0")
        dma("sp", mask_b2[0][:, :], cst_b[:, 128:256], [], ["mask0"], "c0")
        dma("sp", mask_b2[1][:, :], cst_b[:, 256:384], [], ["mask1"], "c0")
        dma("sp", vecs[:, :], vec_fm[:, :], [], ["vecs"], "c0")
        dma("sp", kapbc[:, :], kap_in.partition_broadcast(128), [], ["kapbc"], "c0")
        dma("sp", gb4[0:4, :], gb_in[:, :], [], ["gb4"], "c0")
        dma("sp", m04[0:4, :], m0_in[:, :], [], ["m04"], "c0")
        dma("sp", rope_t[:, :], rope_in[:, :], [], ["rope"], "c0")
        P.add("dve", lambda e: e.memset(epsT[:, :], EPS), w=["eps"])
        ts(gb4[0:4, 16:32], gb4[0:4, 0:16], -1.0, None, ALU.mult, None, r=["gb4"], w=["ngb4"])
        act(csil[:, :], vecs[:, 144:160], AF.Silu, r=["vecs"], w=["csil"])
        cp("dve", sT[:, :], csil[:, :], r=["csil"], w=["sT"])

        P.scope = "mod"
        psm = PS[7]

        def mod_gen(l):
            wv = w_mod[l].rearrange("(k p) n -> p k n", p=128)
            for jt in range(48):
                sc_save = P.scope
                P.scope = "mod"
                slot, wk_ = ws.get(wtile_kc(wv[:, :, jt * 256:(jt + 1) * 256], 16, 256))
                wt = wview(slot, 16, 256)
                for cg in range(2):
                    j = jt * 2 + cg
                    for kc in range(16):
                        mm(psm[:, l * 96 + j:l * 96 + j + 1], wt[:, kc, cg * 128:(cg + 1) * 128], sT[:, kc:kc + 1],
                           kc == 0, kc == 15, r=[wk_, "sT"], w=[psk(7)])
                if jt == 47:
                    tt(modv[:, l * 96:(l + 1) * 96], psm[:, l * 96:(l + 1) * 96],
                       vecs[:, 160 + l * 96:160 + (l + 1) * 96], ALU.add, r=[psk(7), "vecs"], w=[("modv", l)])
                    mA_ = modA[:, l * 96:(l + 1) * 96]
                    mv_ = modv[:, l * 96:(l + 1) * 96]
                    stt(mA_[:, 0:16], mv_[:, 16:32], 1.0, vecs[:, l * 16:(l + 1) * 16], ALU.add, ALU.mult,
                        r=[("modv", l), "vecs"], w=[("modA", l)])
                    stt(mA_[:, 48:64], mv_[:, 64:80], 1.0, vecs[:, 64 + l * 16:64 + (l + 1) * 16], ALU.add, ALU.mult,
                        r=[("modv", l), "vecs"], w=[("modA", l)])
                P.scope = sc_save
                yield

        for _ in mod_gen(0):
            pass
        barrier()

        def norm_phase(src, Acol, Bcol, dst, rkeys_fn, l):
            for tt_ in range(4):
                tsl = slice(tt_ * 512, (tt_ + 1) * 512)
                dma("sp", xtile[:, :, :], src.rearrange("c p t -> p c t")[:, :, tsl],
                    r=[("xT", dc, tt_) for dc in range(16)], w=["xtile"], key="xt")
                psq = PS[tt_ % 2]
                for dc in range(16):
                    sq = sqt[dc % 2]
                    act(sq[:, :], xtile[:, dc, :], AF.Square, r=["xtile"], w=[("sq", dc % 2)])
                    mm(psq[:, :], ones_f[:, :], sq[:, :], dc == 0, dc == 15, r=[("sq", dc % 2), "ones_f"],
                       w=[psk(tt_ % 2)])
                act(rbt[:, :], psq[:, :], AF.Sqrt, r=[psk(tt_ % 2), "eps"], w=["rbt"], bias=epsT[:, 0:1], scale=1.0 / D)
                P.add("dve", lambda e: e.reciprocal(out=rbt[:, :], in_=rbt[:, :]), r=["rbt"], w=["rbt"])
                for dc in range(16):
                    tm = tmpn[dc % 2]
                    tt(tm[:, :], xtile[:, dc, :], rbt[:, :], ALU.mult, r=["xtile", "rbt"], w=[("tmpn", dc % 2)])
                    act(dst[:, dc, tsl], tm[:, :], AF.Identity, r=[("tmpn", dc % 2), ("modA", l)], w=[("hT", tt_)],
                        scale=Acol[:, dc:dc + 1], bias=Bcol[:, dc:dc + 1])

        def rmw_x(psb, dc, tt_, gcol, src, xp, xpk, bank):
            tsl = slice(tt_ * 512, (tt_ + 1) * 512)
            dma("sp", xp[:, :], src[dc][:, tsl], r=[("xT", dc, tt_)], w=[xpk], key=("L",) + xpk)
            stt(xp[:, :], psb[:, :], gcol, xp[:, :], ALU.mult, ALU.add, r=[psk(bank), xpk], w=[xpk])
            dma("act", xT[dc][:, tsl], xp[:, :], r=[xpk], w=[("xT", dc, tt_)], key=("S",) + xpk)

        for l in range(nl):
            first = (l == 0)
            mA = modA[:, l * 96:(l + 1) * 96]
            mv = modv[:, l * 96:(l + 1) * 96]
            win = w_in[l].rearrange("(k p) n -> p k n", p=128)
            P.scope = "norm1"
            norm_phase(x_in if first else xT, mA[:, 0:16], mv[:, 0:16], hT, None, l)
            barrier()
            P.scope = "rtab"
            r8 = slice(l * 8, (l + 1) * 8)
            dma("sp", lgb[:, :], rdec_in[:, r8].partition_broadcast(128), [], ["lgb"], "k_lgb")
            act(lgb[:, :], lgb[:, :], AF.Exp, r=["lgb"], w=["lgb"])
            ts(lgb[:, :], lgb[:, :], -1.0, None, ALU.mult, None, r=["lgb"], w=["lgb"])
            for d in range(2):
                d4 = slice(d * 4, d * 4 + 4)
                ts(rs_row[:, d4], lgb[:, d4], jtab[:, d:d + 1], None, ALU.mult, None, r=["lgb", "jtab"], w=["rs_row"])
                ts(rs_wst[:, d4], lgb[:, d4], jtab[:, 2 + d:3 + d], None, ALU.mult, None, r=["lgb", "jtab"], w=["rs_wst"])
            act(rs_wsc[:, :], rs_row[:, :], AF.Exp, r=["rs_row"], w=["rs_wsc"], scale=-1.0)
            act(rs_row[:, :], rs_row[:, :], AF.Exp, r=["rs_row", "rs_wsc"], w=["rs_row"])
            act(rs_wst[:, :], rs_wst[:, :], AF.Exp, r=["rs_wst"], w=["rs_wst"])
            act(rs_a128[:, :], lgb[:, :], AF.Exp, r=["lgb"], w=["rs_a128"], scale=128.0)
            for d in range(2):
                for h in range(4):
                    i8 = d * 4 + h
                    ts(rs_a128k[:, i8 * 16:(i8 + 1) * 16], kapbc[:, d * 16:(d + 1) * 16], rs_a128[:, i8:i8 + 1], None,
                       ALU.mult, None, r=["kapbc", "rs_a128"], w=["rs_a128k"])

            for grp in range(2):
                if grp == 0:
                    P.scope = "gates"
                    dma("sp", keepT[:, :], keep_in[:, :], [], ["keepT"], "k_keep")
                    gsrc = win[:, :, 8192:8208]
                    slot, wk_ = ws.get([(lambda sl: sl[:, 0:256].rearrange("p (k n) -> p k n", n=16), gsrc)])
                    wg = slot[:, 0:256].rearrange("p (k n) -> p k n", n=16)
                    for d in range(2):
                        for ty in range(2):
                            tyi = d * 2 + ty
                            for tt_ in range(4):
                                tsl = slice(tt_ * 512, (tt_ + 1) * 512)
                                pb = (tyi * 4 + tt_) % 2
                                for kc in range(16):
                                    mm(PS[pb][0:4, :], wg[:, kc, tyi * 4:tyi * 4 + 4], hT[:, kc, tsl], kc == 0, kc == 15,
                                       r=[wk_, ("hT", tt_)], w=[psk(pb)])
                                bcol = l * 4 + tyi
                                if ty == 0:
                                    ts(GI[:, tsl], PS[pb][0:4, :], gb4[0:4, bcol:bcol + 1], None, ALU.add, None,
                                       r=[psk(pb), "gb4"], w=["GI"])
                                else:
                                    act(GE[:, tsl], PS[pb][0:4, :], AF.Exp, r=[psk(pb), "ngb4"], w=["GE"],
                                        scale=-1.0, bias=gb4[0:4, 16 + bcol:17 + bcol])
                        act(GE[:, :], GE[:, :], AF.Ln, r=["GE"], w=["GE"], bias=1.0)
                        P.add("dve", lambda e: e.tensor_tensor_scan(out=GB[:, :], data0=keepT[:, :], data1=GE[:, :],
                                                                     initial=0.0, op0=ALU.mult, op1=ALU.add),
                              r=["keepT", "GE"], w=["GB"])
                        GI3 = GI.rearrange("p (c t) -> p c t", t=128)
                        GE3 = GE.rearrange("p (c t) -> p c t", t=128)
                        GB3 = GB.rearrange("p (c t) -> p c t", t=128)
                        cp("dve", g_Bl[0:4, :].rearrange("p (c o) -> p c o", o=1), GB3[:, :, 127:128], r=["GB"], w=["gBl"])
                        if d == 0:
                            BP = GB
                            BP3 = GB3
                            bpk = "GB"
                        else:
                            tt(GE[:, :], GE[:, :], GB[:, :], ALU.subtract, r=["GE", "GB"], w=["GE"])
                            tt(GE3, GE3, GB3[:, :, 127:128].to_broadcast([4, 16, 128]), ALU.add, r=["GE", "GB"], w=["GE"])
                            BP = GE
                            BP3 = GE3
                            bpk = "GE"
                        tt(GI[:, :], GI[:, :], BP[:, :], ALU.add, r=["GI", bpk], w=["GI"])
                        P.add("dve", lambda e: e.tensor_reduce(out=g_umax[0:4, :], in_=GI3, axis=AX.X, op=ALU.max),
                              r=["GI"], w=["gumax"])
                        order = list(range(16)) if d == 0 else list(range(15, -1, -1))
                        prev = None
                        for c in order:
                            if prev is None:
                                cp("dve", g_mk[0:4, c:c + 1], m04[0:4, l * 2 + d:l * 2 + d + 1], r=["m04"], w=["gmk"])
                            else:
                                tt(g_mk[0:4, c:c + 1], g_m[0:4, prev:prev + 1], kapbc[0:4, d * 16 + c:d * 16 + c + 1],
                                   ALU.mult, r=["gm", "kapbc"], w=["gmk"])
                            tt(g_M[0:4, c:c + 1], g_mk[0:4, c:c + 1], g_umax[0:4, c:c + 1], ALU.max,
                               r=["gmk", "gumax"], w=["gM"])
                            tt(g_m[0:4, c:c + 1], g_M[0:4, c:c + 1], g_Bl[0:4, c:c + 1], ALU.subtract,
                               r=["gM", "gBl"], w=["gm"])
                            prev = c
                        tt(g_ast[0:4, :], g_mk[0:4, :], g_M[0:4, :], ALU.subtract, r=["gmk", "gM"], w=["gast"])
                        act(g_ast[0:4, :], g_ast[0:4, :], AF.Exp, r=["gast"], w=["gast"])
                        tt(g_ast[0:4, :], g_ast[0:4, :], kapbc[0:4, d * 16:(d + 1) * 16], ALU.mult,
                           r=["gast", "kapbc"], w=["gast"])
                        Mb = g_M[0:4, :].rearrange("p (c o) -> p c o", o=1).to_broadcast([4, 16, 128])
                        tt(GI3, GI3, Mb, ALU.subtract, r=["GI", "gM"], w=["GI"])
                        act(GI[:, :], GI[:, :], AF.Exp, r=["GI"], w=["GI"])
                        tt(BP3, BP3, Mb, ALU.subtract, r=[bpk, "gM"], w=[bpk])
                        act(BP[:, :], BP[:, :], AF.Exp, r=[bpk], w=[bpk])
                        for q, (src_, sk) in enumerate(((GI, "GI"), (BP, bpk))):
                            for c in range(16):
                                o0 = q * 64 + c * 4
                                mm(PS[2][:, o0:o0 + 4], src_[0:4, c * 128:(c + 1) * 128], ident_f[0:4, 0:4], True, True,
                                   r=[sk, "ident_f"], w=[psk(2)])
                        cp("dve", gtok[:, d * 128:(d + 1) * 128], PS[2][:, 0:128], r=[psk(2)], w=[("gtok", d)])
                        tt(g_a4[0:4, :].rearrange("p (h c) -> p h c", c=16), sel[0:4, :].rearrange("p (h c) -> p h c", c=16),
                           g_ast[0:4, :].rearrange("p (o c) -> p o c", o=1).to_broadcast([4, 4, 16]), ALU.mult,
                           r=["sel", "gast"], w=["ga4"])
                        mm(PS[3][:, 0:64], ones_f[0:4, :], g_a4[0:4, :], True, True, r=["ga4", "ones_f"], w=[psk(3)])
                        cp("dve", gbc[:, d * 64:(d + 1) * 64], PS[3][:, 0:64], r=[psk(3)], w=[("gbc", d)])
                        g_m3 = g_m[0:4, :].rearrange("p (s two) -> p s two", two=2)
                        dma("sp", oM[:, l, d, :], g_m3[:, :, 1 - d], r=["gm"], w=[("oM", l, d)], key="om",
                            allow_slow_non_contiguous=True)
                    barrier()

                for hh in range(4):
                    base = grp * 4096
                    cols = [base + i * 1024 + hh * 256 for i in range(4)]
                    gsrc = (mng if grp == 0 else rng_)[l:l + 1, hh * 256:(hh + 1) * 256]
                    dma("sp", Gn[:, :], gsrc.partition_broadcast(128), [], ["Gn"], "k_gn")
                    P.add("dve", lambda e: e.memset(vaug[:, :, 256:258], 1.0), w=["vaug1"])
                    P.scope = "proj%d" % grp
                    pend = []
                    gidx = [0]
                    tcnt = [0]

                    def flush(upto):
                        while pend and (upto is None or pend[0][0] <= upto):
                            pend.pop(0)[1]()

                    def emit_tr(j, wi):
                        dstT = qT if wi == 0 else kT
                        dkey = "qT" if wi == 0 else "kT"
                        jsl = slice(j * 128, (j + 1) * 128)
                        pb = 4 + tcnt[0] % 4
                        tcnt[0] += 1
                        for dc in range(2):
                            P.add("pe", lambda e, pb=pb, dc=dc, j=j: e.transpose(
                                PS[pb][:, dc * 128:(dc + 1) * 128], hsum[:, j, dc * 128:(dc + 1) * 128], ident_f[:, :]),
                                r=[("hsum", j), "ident_f"], w=[psk(pb)])
                        cp("dve", dstT[:, :, jsl], PS[pb][:, 0:256].rearrange("p (c t) -> p c t", t=128),
                           r=[psk(pb)], w=[dkey])

                    LAG = 2 if grp == 0 else 4
                    for wi in range(4):
                        slot, wk_ = ws.get(wtile_kc(win[:, :, cols[wi]:cols[wi] + 256], 16, 256))
                        wt = wview(slot, 16, 256)
                        for c in range(16):
                            csl = slice(c * 128, (c + 1) * 128)
                            pb = c % 4
                            for kc in range(16):
                                mm(PS[pb][:, 0:256], hT[:, kc, csl], wt[:, kc, :], kc == 0, kc == 15,
                                   r=[wk_, ("hT", c // 4)], w=[psk(pb)])
                            if wi < 2:
                                sc = 1.0
                                if (grp == 0 and wi == 1) or (grp == 1 and wi == 0):
                                    sc = 1.0 / 16.0
                                act(hsum[:, c, :], PS[pb][:, 0:256], AF.Copy, r=[psk(pb)], w=[("hsum", c)], scale=sc)
                                ready = []
                                if grp == 0:
                                    ready = [c]
                                elif c % 4 == 3:
                                    g4 = slice(c - 3, c + 1)
                                    gk = [("hsum", j) for j in range(c - 3, c + 1)]
                                    for half in range(2):
                                        x1 = hsum[:, g4, half * 128:half * 128 + 64]
                                        x2 = hsum[:, g4, half * 128 + 64:half * 128 + 128]
                                        if half == 0:
                                            cs = rope_t[:, 0:1024].rearrange("p (c e) -> p c e", e=64)[:, g4, :]
                                            sn = rope_t[:, 1024:2048].rearrange("p (c e) -> p c e", e=64)[:, g4, :]
                                        else:
                                            cs = rope_t[:, 2048:2112].rearrange("p (o e) -> p o e", o=1).to_broadcast([128, 4, 64])
                                            sn = rope_t[:, 2112:2176].rearrange("p (o e) -> p o e", o=1).to_broadcast([128, 4, 64])
                                        rA = ropeA[:, 0:4, :]
                                        rB = ropeB[:, 0:4, :]
                                        tt(rA, x2, sn, ALU.mult, r=gk + ["rope"], w=["og"])
                                        tt(rB, x1, sn, ALU.mult, r=gk + ["rope", "og"], w=["og"])
                                        tt(x1, x1, cs, ALU.mult, r=gk + ["rope", "og"], w=gk)
                                        tt(x1, x1, rA, ALU.subtract, r=gk + ["og"], w=gk)
                                        tt(x2, x2, cs, ALU.mult, r=gk + ["rope"], w=gk)
                                        tt(x2, x2, rB, ALU.add, r=gk + ["og"], w=gk)
                                    ready = list(range(c - 3, c + 1))
                                if ready:
                                    if wi == 1:
                                        rs_ = slice(ready[0], ready[-1] + 1)
                                        cp("act", ktok[:, rs_, :], hsum[:, rs_, :], r=[("hsum", j) for j in ready], w=["ktok"])
                                    for j in ready:
                                        pend.append((gidx[0] + LAG, lambda j=j, wi=wi: emit_tr(j, wi)))
                            elif wi == 2:
                                cp("dve", vaug[:, c, 0:256], PS[pb][:, 0:256], r=[psk(pb)], w=[("vaug", c)])
                            else:
                                act(og[:, c, :], PS[pb][:, 0:256], AF.Sigmoid if grp == 0 else AF.Silu,
                                    r=[psk(pb)], w=["og"])
                            gidx[0] += 1
                            flush(gidx[0])
                    flush(None)
                    P.scope = "scan%d" % grp
                    cur = [0, 0]
                    for d in range(2):
                        src = (initC if grp == 0 else initS)[l, d, hh].rearrange("(c p) e -> p c e", p=128)
                        ne = 257 if grp == 0 else 256
                        dma("sp", Cst[d][0][:, :, 0:ne], src, [], [("Cst", d, 0)], ("ci", d))
                    written = [False] * 16
                    for step in range(16):
                        for d in range(2):
                            c = step if d == 0 else 15 - step
                            csl = slice(c * 128, (c + 1) * 128)
                            pp = cur[d]
                            ne = 257 if grp == 0 else 256
                            if grp == 0:
                                gt = gtok[:, d * 128:(d + 1) * 128]
                                wsc = gt[:, c * 4 + hh:c * 4 + hh + 1]
                                wst = wsc
                                flo = gt[:, 64 + c * 4 + hh:64 + c * 4 + hh + 1]
                                a_in = gbc[:, d * 64 + hh * 16 + c:d * 64 + hh * 16 + c + 1]
                                a_st = a_in
                                skeys = [("gtok", d), ("gbc", d)]
                            else:
                                i8 = d * 4 + hh
                                wsc = rs_wsc[:, i8:i8 + 1]
                                wst = rs_wst[:, i8:i8 + 1]
                                row = rs_row[:, i8:i8 + 1]
                                a_in = kapbc[:, d * 16 + c:d * 16 + c + 1]
                                a_st = rs_a128k[:, i8 * 16 + c:i8 * 16 + c + 1]
                                skeys = ["rs_wsc", "rs_wst", "rs_row", "kapbc", "rs_a128k"]
                            act(Cbf[d][:, :, 0:ne], Cst[d][pp][:, :, 0:ne], AF.Copy, r=[("Cst", d, pp)] + skeys,
                                w=[("Cbf", d)], scale=a_in)
                            for dc in range(2):
                                mm(PS[d][:, 0:128], kT[:, dc, csl], qT[:, dc, csl], dc == 0, dc == 1,
                                   r=["kT", "qT"], w=[psk(d)])
                            stt(STb[d][:, :], PS[d][:, 0:128], wsc, mask_b2[d][:, :], ALU.mult, ALU.mult,
                                r=[psk(d), "mask%d" % d] + skeys, w=[("STb", d)])
                            for dc in range(2):
                                mm(PS[2 + d][:, 0:ne], qT[:, dc, csl], Cbf[d][:, dc, 0:ne], dc == 0, False,
                                   r=["qT", ("Cbf", d)], w=[psk(2 + d)])
                            mm(PS[2 + d][:, 0:ne], STb[d][:, :], vaug[:, c, 0:ne], False, True,
                               r=[("STb", d), ("vaug", c), "vaug1"], w=[psk(2 + d)])
                            if grp == 0:
                                den = small[:, d * 2:d * 2 + 1]
                                act(den, PS[2 + d][:, 256:257], AF.Abs, r=[psk(2 + d)], w=[("den", d)])
                                tt(den, den, flo, ALU.max, r=[("den", d)] + skeys, w=[("den", d)])
                                P.add("dve", lambda e, den=den: e.reciprocal(out=den, in_=den), r=[("den", d)], w=[("den", d)])
                                rsc = den
                                rk = [("den", d)]
                            else:
                                rsc = row
                                rk = skeys
                            if not written[c]:
                                act(hsum[:, c, :], PS[2 + d][:, 0:256], AF.Copy, r=[psk(2 + d)] + rk, w=[("hsum", c)],
                                    scale=rsc)
                                written[c] = True
                                fin = False
                            else:
                                stt(hsum[:, c, :], PS[2 + d][:, 0:256], rsc, hsum[:, c, :], ALU.mult, ALU.add,
                                    r=[psk(2 + d), ("hsum", c)] + rk, w=[("hsum", c)])
                                fin = True
                            ts(wkb[d][:, :], ktok[:, c, :], wst, None, ALU.mult, None, r=["ktok"] + skeys, w=[("wk", d)],
                               eng="pool")
                            for dc in range(2):
                                mm(PS[4 + dc][:, 0:ne], wkb[d][:, dc * 128:(dc + 1) * 128], vaug[:, c, 0:ne], True, True,
                                   r=[("wk", d), ("vaug", c), "vaug1"], w=[psk(4 + dc)])
                            for dc in range(2):
                                stt(Cst[d][1 - pp][:, dc, 0:ne], Cst[d][pp][:, dc, 0:ne], a_st, PS[4 + dc][:, 0:ne],
                                    ALU.mult, ALU.add, r=[("Cst", d, pp), psk(4 + dc)] + skeys, w=[("Cst", d, 1 - pp)])
                            cur[d] = 1 - pp
                            if (d == 0 and c % 2 == 1) or (d == 1 and c % 2 == 0):
                                sq_ = c // 2
                                if grp == 0:
                                    dst = oC[sq_, l, d, hh].rearrange("(c p) e -> p c e", p=128)
                                else:
                                    dst = oS[sq_, l, d, hh].rearrange("(c p) e -> p c e", p=128)
                                dma("sp", dst, Cst[d][1 - pp][:, :, 0:ne], r=[("Cst", d, 1 - pp)],
                                    w=[("ost", grp, l, d, hh, sq_)], key=("so", d, 1 - pp))
                            if fin:
                                i2 = c % 2
                                ssq = small[:, 4 + i2:5 + i2]
                                act(t1[i2][:, :], hsum[:, c, :], AF.Square, r=[("hsum", c)], w=[("t1", i2), ("ssq", i2)],
                                    accum_out=ssq)
                                act(ssq, ssq, AF.Sqrt, r=[("ssq", i2), "eps"], w=[("ssq", i2)], bias=epsT[:, 0:1],
                                    scale=1.0 / 256.0)
                                P.add("dve", lambda e, ssq=ssq: e.reciprocal(out=ssq, in_=ssq), r=[("ssq", i2)], w=[("ssq", i2)])
                                stt(t1[i2][:, :], hsum[:, c, :], ssq, og[:, c, :], ALU.mult, ALU.mult,
                                    r=[("hsum", c), ("ssq", i2), "og"], w=[("t1", i2)])
                                tt(ytok[i2][:, :], t1[i2][:, :], Gn[:, :], ALU.mult, r=[("t1", i2), "Gn"], w=[("ytok", i2)])
                                psy = PS[6 + i2][:, 0:128].bitcast(BF16).rearrange("p (c t) -> p c t", t=128)
                                for dc in range(2):
                                    P.add("pe", lambda e, psy=psy, dc=dc, i2=i2: e.transpose(
                                        psy[:, dc, :], ytok[i2][:, dc * 128:(dc + 1) * 128], ident_b[:, :]),
                                        r=[("ytok", i2), "ident_b"], w=[psk(6 + i2)])
                                cp("act", yT[:, hh * 2:hh * 2 + 2, csl], psy, r=[psk(6 + i2)], w=["yT"])
                barrier()
                P.scope = "outproj"
                wo = w_out[l, grp * 1024:(grp + 1) * 1024, :].rearrange("(k p) n -> p k n", p=128)
                n = 0
                for jt in range(8):
                    slot, wk_ = ws.get(wtile_kc(wo[:, :, jt * 256:(jt + 1) * 256], 8, 256))
                    wt = wview(slot, 8, 256)
                    for cg in range(2):
                        dc_ = jt * 2 + cg
                        for tt_ in range(4):
                            tsl = slice(tt_ * 512, (tt_ + 1) * 512)
                            pb = n % 8
                            for kc in range(8):
                                mm(PS[pb][:, :], wt[:, kc, cg * 128:(cg + 1) * 128], yT[:, kc, tsl], kc == 0, kc == 7,
                                   r=[wk_, "yT"], w=[psk(pb)])
                            rmw_x(PS[pb], dc_, tt_, mv[:, 32 + dc_:33 + dc_], x_in if (first and grp == 0) else xT,
                                  xp_o[n % NXO], ("xpo", n % NXO), pb)
                            n += 1
                barrier()

            P.scope = "norm2"
            norm_phase(xT, mA[:, 48:64], mv[:, 48:64], h2T, None, l)
            barrier()
            wgu = w_gu[l].rearrange("(k p) n -> p k n", p=128)
            mgen = mod_gen(l + 1) if l + 1 < nl else None

            def pump():
                if mgen is not None:
                    next(mgen, None)

            for half in range(2):
                n = 0
                P.scope = "ffn_gu"
                for hc in range(22):
                    hg = half * 22 + hc
                    parts = [
                        (lambda sl: sl[:, 0:4096].rearrange("p (k n) -> p k n", n=256)[:, :, 0:128],
                         wgu[:, :, hg * 128:(hg + 1) * 128]),
                        (lambda sl: sl[:, 0:4096].rearrange("p (k n) -> p k n", n=256)[:, :, 128:256],
                         wgu[:, :, DFF + hg * 128:DFF + (hg + 1) * 128]),
                    ]
                    slot, wk_ = ws.get(parts)
                    wt = wview(slot, 16, 256)
                    for tt_ in range(4):
                        tsl = slice(tt_ * 512, (tt_ + 1) * 512)
                        pa = (2 * n) % 6
                        pu = (2 * n + 1) % 6
                        for kc in range(16):
                            mm(PS[pa][:, :], wt[:, kc, 0:128], h2T[:, kc, tsl], kc == 0, kc == 15,
                               r=[wk_, ("hT", tt_)], w=[psk(pa)])
                        for kc in range(16):
                            mm(PS[pu][:, :], wt[:, kc, 128:256], h2T[:, kc, tsl], kc == 0, kc == 15,
                               r=[wk_, ("hT", tt_)], w=[psk(pu)])
                        sa = sat[n % 2]
                        act(sa[:, :], PS[pa][:, :], AF.Silu, r=[psk(pa)], w=[("sat", n % 2)])
                        tt(actT[:, hc, tsl], sa[:, :], PS[pu][:, :], ALU.mult, r=[("sat", n % 2), psk(pu)], w=["actT"])
                        n += 1
                    pump()
                P.scope = "ffn_down"
                wdn = w_down[l, half * 22 * 128:(half + 1) * 22 * 128, :].rearrange("(k p) n -> p k n", p=128)
                n = 0
                for dc_ in range(16):
                    slot, wk_ = ws.get(wtile_kc(wdn[:, :, dc_ * 128:(dc_ + 1) * 128], 22, 128))
                    wt = wview(slot, 22, 128)
                    for tt_ in range(4):
                        tsl = slice(tt_ * 512, (tt_ + 1) * 512)
                        pb = n % 6
                        for kc in range(22):
                            mm(PS[pb][:, :], wt[:, kc, :], actT[:, kc, tsl], kc == 0, kc == 21,
                               r=[wk_, "actT"], w=[psk(pb)])
                        rmw_x(PS[pb], dc_, tt_, mv[:, 80 + dc_:81 + dc_], xT, xp_f[n % NXF], ("xpf", n % NXF), pb)
                        n += 1
                    pump()
            if mgen is not None:
                for _ in mgen:
                    pass
            barrier()

        P.scope = "final"
        fg = vecs[:, 128:144]
        srcx = xT if nl > 0 else x_in
        for tt_ in range(4):
            tsl = slice(tt_ * 512, (tt_ + 1) * 512)
            dma("sp", xtile[:, :, :], srcx.rearrange("c p t -> p c t")[:, :, tsl], r=[("xT", dc, tt_) for dc in range(16)], w=["xtile"], key="xt")
            psq = PS[tt_ % 2]
            for dc in range(16):
                sq = sqt[dc % 2]
                act(sq[:, :], xtile[:, dc, :], AF.Square, r=["xtile"], w=[("sq", dc % 2)])
                mm(psq[:, :], ones_f[:, :], sq[:, :], dc == 0, dc == 15, r=[("sq", dc % 2), "ones_f"], w=[psk(tt_ % 2)])
            act(rbt[:, :], psq[:, :], AF.Sqrt, r=[psk(tt_ % 2), "eps"], w=["rbt"], bias=epsT[:, 0:1], scale=1.0 / D)
            P.add("dve", lambda e: e.reciprocal(out=rbt[:, :], in_=rbt[:, :]), r=["rbt"], w=["rbt"])
            for dc in range(16):
                stt(xtile[:, dc, :], xtile[:, dc, :], fg[:, dc:dc + 1], rbt[:, :], ALU.mult, ALU.mult,
                    r=["xtile", "rbt", "vecs"], w=["xtile"])
            dma("sp", yT_out.rearrange("c p t -> p c t")[:, :, tsl], xtile[:, :, :], r=["xtile"], w=[("yout", tt_)], key="xt")

    P.dry = True
    program()
    P.dry = False
    program()
    P.emit(nc)
    return nc


_NC_CACHE = {}


def _consts():
    cf = np.zeros((128, 1536), np.float32)
    cf[:, 0:128] = np.eye(128, dtype=np.float32)
    cf[:, 128:256] = 1.0
    selm = np.zeros((4, 4, 16), np.float32)
    for h in range(4):
        selm[h, h, :] = 1.0
    cf[0:4, 256:320] = selm.reshape(4, 64)
    t = np.arange(128, dtype=np.float32)
    cf[:, 320] = t + 1.0
    cf[:, 321] = 128.0 - t
    cf[:, 322] = 127.0 - t
    cf[:, 323] = t
    cb = np.zeros((128, 384), np.float32)
    cb[:, 0:128] = np.eye(128)
    s = np.arange(128)[:, None]
    tt = np.arange(128)[None, :]
    cb[:, 128:256] = (tt >= s)
    cb[:, 256:384] = (tt <= s)
    keep = np.ones((4, T), np.float32)
    keep[:, ::128] = 0.0
    return cf, cb.astype(ml_dtypes.bfloat16), keep


def _rope(latent):
    out = np.zeros((128, 2 * 1024 + 128), np.float32)
    if not latent:
        out[:, 0:1024] = 1.0
        out[:, 2048:2112] = 1.0
        return out
    inv = (10000.0 ** (-np.arange(64, dtype=np.float32) / 64.0)).astype(np.float32)
    p = np.arange(128)
    c = np.arange(16)
    r = (2 * c[None, :] + (p[:, None] // 64)).astype(np.float32)
    col = (p % 64).astype(np.float32)
    ang_r = (r[:, :, None] * inv[None, None, :]).astype(np.float32)
    ang_c = (col[:, None] * inv[None, :]).astype(np.float32)
    out[:, 0:1024] = np.cos(ang_r).reshape(128, 1024)
    out[:, 1024:2048] = np.sin(ang_r).reshape(128, 1024)
    out[:, 2048:2112] = np.cos(ang_c)
    out[:, 2112:2176] = np.sin(ang_c)
    return out


def kernel(x_prompt, x_sample, c, state_mlstm_C, state_mlstm_n, state_mlstm_m, state_ret_S,
           c_ctx, w_mod, b_mod, norm1_g, w_in, mlstm_gate_b, ret_decay, mlstm_norm_g, ret_norm_g,
           w_out, norm2_g, w_gu, w_down, final_norm_g, _nl=NL, _cores=None, _trace=False, _scopes=False):
    f = lambda a: np.ascontiguousarray(np.asarray(a, dtype=np.float32))
    x_prompt, x_sample, c, c_ctx = f(x_prompt), f(x_sample), f(c), f(c_ctx)
    nl = _nl
    if (nl, _scopes) not in _NC_CACHE:
        _NC_CACHE[(nl, _scopes)] = build(nl, _scopes)
    nc = _NC_CACHE[(nl, _scopes)]
    cf, cb, keep = _consts()

    def fm(v):
        v = np.asarray(v, np.float32)
        return v.reshape(v.shape[:-1] + (16, 128))

    shared = {
        "w_mod": f(w_mod), "w_in": f(w_in), "w_out": f(w_out), "w_gu": f(w_gu), "w_down": f(w_down),
        "mnorm_g": f(mlstm_norm_g), "rnorm_g": f(ret_norm_g), "cst_f": cf, "cst_b": cb, "keep": keep,
        "rdec": f(ret_decay).reshape(1, 32),
    }
    gbh = np.zeros((4, 32), np.float32)
    gbh[:, 0:16] = np.transpose(f(mlstm_gate_b), (2, 0, 1)).reshape(4, 16)
    shared["gb"] = gbh
    n1 = np.transpose(fm(norm1_g), (2, 0, 1)).reshape(128, 64)
    n2 = np.transpose(fm(norm2_g), (2, 0, 1)).reshape(128, 64)
    fgm = fm(final_norm_g).T
    bm = np.transpose(f(b_mod).reshape(4, 96, 128), (2, 0, 1)).reshape(128, 384)
    rope_lat, rope_ctx = _rope(True), _rope(False)
    in_maps = []
    for core in range(8):
        m = dict(shared)
        latent = core < 4
        if latent:
            b = core
            xs = x_sample[b]
            cv = c[b]
            Caug = np.concatenate([f(state_mlstm_C)[b], f(state_mlstm_n)[b][..., None]], axis=-1)
            S0 = f(state_ret_S)[b]
            m0 = f(state_mlstm_m)[b]
            kap = np.ones((1, 32), np.float32)
        else:
            j = core - 4
            xs = x_prompt[8 * j:8 * j + 8].reshape(T, D)
            cv = c_ctx
            Caug = np.zeros((NL, 2, 4, 256, 257), np.float32)
            S0 = np.zeros((NL, 2, 4, 256, 256), np.float32)
            m0 = np.zeros((NL, 2, 4), np.float32)
            kap = np.ones((1, 32), np.float32)
            kap[0, 0:16:2] = 0.0
            kap[0, 17:32:2] = 0.0
            kap[0, 0] = 1.0
            kap[0, 31] = 1.0
        m["x_in"] = np.ascontiguousarray(xs.T.reshape(16, 128, T))
        vec = np.zeros((128, 544), np.float32)
        vec[:, 0:64] = n1
        vec[:, 64:128] = n2
        vec[:, 128:144] = fgm
        vec[:, 144:160] = cv.reshape(16, 128).T
        vec[:, 160:544] = bm
        m["vec_fm"] = vec
        m["m0"] = np.ascontiguousarray(np.transpose(m0, (2, 0, 1)).reshape(4, 8))
        m["kap"] = kap
        m["init_Caug"] = np.ascontiguousarray(Caug)
        m["init_S"] = np.ascontiguousarray(S0)
        m["rope"] = rope_lat if latent else rope_ctx
        in_maps.append(m)
    if _cores is not None:
        res = run_bass_kernel_spmd(nc, [in_maps[i] for i in _cores], core_ids=list(range(len(_cores))), trace=_trace)
        return res
    res = run_bass_kernel_spmd(nc, in_maps, core_ids=list(range(8)))
    R = res.results
    y_sample = np.stack([R[b]["yT"].reshape(D, T).T for b in range(4)], axis=0)
    y_prompt = np.concatenate([R[4 + j]["yT"].reshape(D, T).T.reshape(8, 256, D) for j in range(4)], axis=0)
    oCa = np.concatenate([R[4 + j]["o_Caug"] for j in range(4)], axis=0)
    new_C = np.ascontiguousarray(oCa[..., :256])
    new_n = np.ascontiguousarray(oCa[..., 256])
    new_S = np.concatenate([R[4 + j]["o_S"] for j in range(4)], axis=0)
    new_m = np.concatenate([np.transpose(R[4 + j]["o_m"], (3, 1, 2, 0)) for j in range(4)], axis=0)
    return (y_prompt.astype(np.float32), y_sample.astype(np.float32), new_C.astype(np.float32),
            new_n.astype(np.float32), np.ascontiguousarray(new_m).astype(np.float32), new_S.astype(np.float32))
```

```python
import numpy as np
import ml_dtypes
import concourse.bass as bass
import concourse.mybir as mybir
from concourse.bass_utils import run_bass_kernel_spmd

F32 = mybir.dt.float32
BF16 = mybir.dt.bfloat16
AF = mybir.ActivationFunctionType
ALU = mybir.AluOpType
AX = mybir.AxisListType

D = 2048
T = 2048
NCH = 16
DFF = 5632
DIN = 8208
EPS = 1e-6
NL = 4
PH = "*PH*"


class Op:
    __slots__ = ("eng", "fn", "deps", "dma_key", "needs_inc", "inc_val", "scope")


class Prog:
    ENGS = ("pe", "act", "dve", "pool", "sp")

    def __init__(self):
        self.ops = {e: [] for e in self.ENGS}
        self.last_w = {}
        self.readers = {}
        self.dma_cnt = {}
        self.dry = False
        self.scope = None
        self.use_scopes = False

    def add(self, eng, fn, r=(), w=(), dma=None, phase=True):
        if self.dry:
            return None
        op = Op()
        op.eng = eng
        op.fn = fn
        op.dma_key = dma
        op.needs_inc = False
        op.inc_val = 0
        op.scope = self.scope
        deps = []
        r = list(r)
        w = list(w)
        if phase:
            r.append(PH)

        def dep(o):
            if o is None:
                return
            if o.dma_key is not None:
                deps.append((o, self.dma_cnt[o.dma_key]))
            else:
                if o.eng == "pe" and eng == "pe" and dma is None:
                    return
                o.needs_inc = True
                deps.append((o, None))

        for k in r:
            dep(self.last_w.get(k))
        for k in w:
            dep(self.last_w.get(k))
            for o in self.readers.get(k, ()):
                dep(o)
        if dma is not None:
            self.dma_cnt[dma] = self.dma_cnt.get(dma, 0) + 16
        for k in r:
            self.readers.setdefault(k, []).append(op)
        for k in w:
            self.last_w[k] = op
            self.readers[k] = []
        op.deps = deps
        self.ops[eng].append(op)
        return op

    def emit(self, nc):
        for e, ops in self.ops.items():
            n = 0
            for op in ops:
                if op.dma_key is None and op.needs_inc:
                    n += 1
                    op.inc_val = n
        sems = {}
        for e in self.ENGS:
            sems[("c", e)] = nc.alloc_semaphore("c_" + e)
        for i, k in enumerate(self.dma_cnt):
            sems[("d", k)] = nc.alloc_semaphore("d_%d" % i)
        prog = self

        def run(name, eng):
            seen = {}
            cur_scope = None
            for op in prog.ops[name]:
                if prog.use_scopes and op.scope != cur_scope:
                    if cur_scope is not None:
                        nc.pop_named_scope(cur_scope)
                    if op.scope is not None:
                        nc.push_named_scope(op.scope)
                    cur_scope = op.scope
                need = {}
                for (o, v) in op.deps:
                    if o.dma_key is not None:
                        key = ("d", o.dma_key)
                        val = v
                    else:
                        key = ("c", o.eng)
                        val = o.inc_val
                    if val > need.get(key, 0):
                        need[key] = val
                for key, val in need.items():
                    if seen.get(key, 0) >= val:
                        continue
                    eng.wait_ge(sems[key], val)
                    seen[key] = val
                ins = op.fn(eng)
                if op.dma_key is not None:
                    ins.then_inc(sems[("d", op.dma_key)], 16)
                elif op.needs_inc:
                    ins.then_inc(sems[("c", name)], 1)
            if prog.use_scopes and cur_scope is not None:
                nc.pop_named_scope(cur_scope)
            if name == "sp":
                for k, tot in prog.dma_cnt.items():
                    if seen.get(("d", k), 0) < tot:
                        eng.wait_ge(sems[("d", k)], tot)

        with nc.Block() as block:
            @block.tensor
            def _(e):
                run("pe", e)

            @block.scalar
            def _(e):
                run("act", e)

            @block.vector
            def _(e):
                run("dve", e)

            @block.gpsimd
            def _(e):
                run("pool", e)

            @block.sync
            def _(e):
                run("sp", e)


class Arena:
    def __init__(self, ar, nwords):
        self.ar = ar
        self.nbytes = nwords * 4

    def view(self, off, nelem, dtype, parts=128):
        assert off % 4 == 0
        nb = nelem * (4 if dtype == F32 else 2)
        nb4 = (nb + 3) // 4 * 4
        assert off + nb4 <= self.nbytes, (off, nb4, self.nbytes)
        ap = self.ar[0:parts, off // 4:(off + nb4) // 4]
        if dtype != F32:
            ap = ap.bitcast(dtype)
            if nb4 != nb:
                ap = ap[:, 0:nelem]
        return ap


def build(nl=NL, scopes=False):
    nc = bass.Bass("TRN2", target_bir_lowering=False)
    P = Prog()
    P.use_scopes = scopes

    def din(name, shape, dtype=F32):
        return nc.dram_tensor(name, shape, dtype, kind="ExternalInput").ap()

    def dout(name, shape, dtype=F32):
        return nc.dram_tensor(name, shape, dtype, kind="ExternalOutput").ap()

    x_in = din("x_in", [16, 128, T])
    vec_fm = din("vec_fm", [128, 544])
    gb_in = din("gb", [4, 32])
    m0_in = din("m0", [4, 8])
    rdec_in = din("rdec", [1, 32])
    kap_in = din("kap", [1, 32])
    initC = din("init_Caug", [NL, 2, 4, 256, 257])
    initS = din("init_S", [NL, 2, 4, 256, 256])
    w_mod = din("w_mod", [NL, D, 6 * D])
    w_in = din("w_in", [NL, D, DIN])
    w_out = din("w_out", [NL, D, D])
    w_gu = din("w_gu", [NL, D, 2 * DFF])
    w_down = din("w_down", [NL, DFF, D])
    mng = din("mnorm_g", [NL, 1024])
    rng_ = din("rnorm_g", [NL, 1024])
    cst_f = din("cst_f", [128, 1536])
    cst_b = din("cst_b", [128, 384], BF16)
    keep_in = din("keep", [4, T])
    rope_in = din("rope", [128, 2 * 16 * 64 + 2 * 64])

    yT_out = dout("yT", [16, 128, T])
    oC = dout("o_Caug", [8, NL, 2, 4, 256, 257])
    oS = dout("o_S", [8, NL, 2, 4, 256, 256])
    oM = dout("o_m", [4, NL, 2, 8])
    xT = nc.dram_tensor("xT_scr", [16, 128, T], F32, kind="Internal").ap()

    NW = 52900
    AR = nc.alloc_sbuf_tensor("arena", [128, NW], F32)
    A = Arena(AR, NW)
    off = [0]

    def alloc(nelem, dtype, parts=128):
        nb = (nelem * (4 if dtype == F32 else 2) + 31) // 32 * 32
        o = off[0]
        off[0] += nb
        return A.view(o, nelem, dtype, parts), o

    ident_b, _ = alloc(128, BF16)
    mask_b2 = [alloc(128, BF16)[0], alloc(128, BF16)[0]]
    ident_f, _ = alloc(128, F32)
    ones_f, _ = alloc(128, F32)
    vecs, _ = alloc(544, F32)
    modv, _ = alloc(4 * 96, F32)
    modA, _ = alloc(4 * 96, F32)
    sT, _ = alloc(16, BF16)
    csil, _ = alloc(16, F32)
    kapbc, _ = alloc(32, F32)
    gb4, _ = alloc(32, F32)
    m04, _ = alloc(8, F32)
    lgb, _ = alloc(8, F32)
    rs_row, _ = alloc(8, F32)
    rs_wsc, _ = alloc(8, F32)
    rs_wst, _ = alloc(8, F32)
    rs_a128, _ = alloc(8, F32)
    rs_a128k, _ = alloc(128, F32)
    jtab, _ = alloc(4, F32)
    gtok, _ = alloc(256, F32)
    gbc, _ = alloc(128, F32)
    g_umax, _ = alloc(16, F32)
    g_M, _ = alloc(16, F32)
    g_mk, _ = alloc(16, F32)
    g_m, _ = alloc(16, F32)
    g_Bl, _ = alloc(16, F32)
    g_ast, _ = alloc(16, F32)
    g_a4, _ = alloc(64, F32)
    sel, _ = alloc(64, F32)
    rope_t, _ = alloc(2 * 1024 + 128, F32)
    Gn, _ = alloc(256, F32)
    small, _ = alloc(16, F32)
    epsT, _ = alloc(1, F32)
    wslot = [alloc(16 * 256, BF16)[0], alloc(16 * 256, BF16)[0]]
    BIG = off[0]
    o_hT = BIG
    o_yT = o_hT + 65536
    o_head = o_yT + 32768
    o_qT = o_head
    o_kT = o_qT + 8192
    o_ktok = o_kT + 8192
    o_vaug = o_ktok + 8192
    o_og = o_vaug + 8256
    o_hsum = o_og + 8192
    o_scan = o_hsum + 16384
    o_Cst = o_scan
    o_Cbf = o_Cst + 8224
    o_STb = o_Cbf + 2064
    o_wk = o_STb + 512
    o_ytok = o_wk + 1024
    o_t1 = o_ytok + 1024
    o_end = o_t1 + 2048
    assert o_end <= NW * 4, (o_end, NW * 4)

    hT = A.view(o_hT, 16 * T, BF16).rearrange("p (c t) -> p c t", t=T)
    yT = A.view(o_yT, 8 * T, BF16).rearrange("p (c t) -> p c t", t=T)
    qT = A.view(o_qT, 2 * T, BF16).rearrange("p (c t) -> p c t", t=T)
    kT = A.view(o_kT, 2 * T, BF16).rearrange("p (c t) -> p c t", t=T)
    ktok = A.view(o_ktok, 16 * 256, BF16).rearrange("p (c e) -> p c e", e=256)
    vaug = A.view(o_vaug, 16 * 258, BF16).rearrange("p (c e) -> p c e", e=258)
    og = A.view(o_og, 16 * 256, BF16).rearrange("p (c e) -> p c e", e=256)
    hsum = A.view(o_hsum, 16 * 256, F32).rearrange("p (c e) -> p c e", e=256)
    ropeA = A.view(o_og, 1024, F32).rearrange("p (c e) -> p c e", e=64)
    ropeB = A.view(o_og + 4096, 1024, F32).rearrange("p (c e) -> p c e", e=64)
    Cst = [[A.view(o_Cst + (d * 2 + pp) * 2056, 514, F32).rearrange("p (c e) -> p c e", e=257)
            for pp in range(2)] for d in range(2)]
    Cbf = [A.view(o_Cbf + d * 1032, 516, BF16).rearrange("p (c e) -> p c e", e=258) for d in range(2)]
    STb = [A.view(o_STb + d * 256, 128, BF16) for d in range(2)]
    wkb = [A.view(o_wk + d * 512, 256, BF16) for d in range(2)]
    ytok = [A.view(o_ytok + i * 512, 256, BF16) for i in range(2)]
    t1 = [A.view(o_t1 + i * 1024, 256, F32) for i in range(2)]
    xtile = A.view(o_head, 16 * 512, F32).rearrange("p (c t) -> p c t", t=512)
    sqt = [A.view(o_head + 32768 + i * 2048, 512, BF16) for i in range(2)]
    ones_b = A.view(o_head + 32768 + 10240, 128, BF16)
    rbt = A.view(o_head + 32768 + 4096, 512, F32)
    tmpn = [A.view(o_head + 32768 + 6144 + i * 2048, 512, F32) for i in range(2)]
    GI = A.view(o_head, T, F32, parts=4)
    GE = A.view(o_head + 8192, T, F32, parts=4)
    GB = A.view(o_head + 16384, T, F32, parts=4)
    keepT = A.view(o_head + 24576, T, F32, parts=4)
    NXO = 16
    xp_o = [A.view(o_head + i * 2048, 512, F32) for i in range(NXO)]
    h2T = hT
    actT = A.view(o_yT, 22 * T, BF16).rearrange("p (c t) -> p c t", t=T)
    o_ffx = o_yT + 22 * T * 2
    NXF = 5
    xp_f = [A.view(o_ffx + i * 2048, 512, F32) for i in range(NXF)]
    sat = [A.view(o_ffx + NXF * 2048 + i * 2048, 512, F32) for i in range(2)]
    assert o_ffx + NXF * 2048 + 4096 <= NW * 4

    PS = [nc.alloc_psum_tensor("ps%d" % i, [128, 512], F32) for i in range(8)]

    def psk(b):
        return ("ps", b)

    def barrier():
        P.add("dve", lambda e: e.memset(small[:, 15:16], 0.0), w=[PH, "small15"], phase=False)

    def dma(q, out, in_, r, w, key, phase=True, **kw):
        return P.add(q, lambda e: e.dma_start(out=out, in_=in_, **kw), r=r, w=w, dma=key, phase=phase)

    def mm(out, lhsT, rhs, start, stop, r, w):
        P.add("pe", lambda e: e.matmul(out, lhsT, rhs, start=start, stop=stop), r=r, w=w)

    def act(out, in_, func, r, w, bias=None, scale=None, accum_out=None):
        kw = {}
        if bias is not None:
            kw["bias"] = bias
        if scale is not None:
            kw["scale"] = scale
        if accum_out is not None:
            kw["accum_out"] = accum_out
        P.add("act", lambda e: e.activation(out=out, in_=in_, func=func, **kw), r=r, w=w)

    def tt(out, in0, in1, op, r, w, eng="dve"):
        P.add(eng, lambda e: e.tensor_tensor(out=out, in0=in0, in1=in1, op=op), r=r, w=w)

    def ts(out, in0, s1, s2, op0, op1, r, w, eng="dve"):
        if s2 is None:
            P.add(eng, lambda e: e.tensor_scalar(out=out, in0=in0, scalar1=s1, scalar2=None, op0=op0), r=r, w=w)
        else:
            P.add(eng, lambda e: e.tensor_scalar(out=out, in0=in0, scalar1=s1, scalar2=s2, op0=op0, op1=op1), r=r, w=w)

    def stt(out, in0, scalar, in1, op0, op1, r, w):
        P.add("dve", lambda e: e.scalar_tensor_tensor(out=out, in0=in0, scalar=scalar, in1=in1, op0=op0, op1=op1), r=r, w=w)

    def cp(eng, out, in_, r, w):
        if eng == "act":
            P.add("act", lambda e: e.copy(out=out, in_=in_), r=r, w=w)
        else:
            P.add(eng, lambda e: e.tensor_copy(out=out, in_=in_), r=r, w=w)

    class WS:
        def __init__(self):
            self.descs = []
            self.n = 0
            self.issued = 0

        def _issue(self, i):
            parts = self.descs[i]
            s = i % 2
            for (dst_fn, src) in parts:
                dst = dst_fn(wslot[s])
                dma("pool", dst, src, r=[], w=[("w", s)], key=("w", s), phase=False)
            self.issued = i + 1

        def get(self, parts):
            i = self.n
            self.n += 1
            if P.dry:
                self.descs.append(parts)
                return wslot[i % 2], ("w", i % 2)
            while self.issued <= min(i + 1, len(self.descs) - 1):
                self._issue(self.issued)
            return wslot[i % 2], ("w", i % 2)

    ws = WS()

    def wtile_kc(src3, kcn, ncols):
        return [(lambda sl: sl[:, 0:kcn * ncols].rearrange("p (k n) -> p k n", n=ncols), src3)]

    def wview(slot, kcn, ncols):
        return slot[:, 0:kcn * ncols].rearrange("p (k n) -> p k n", n=ncols)

    def xsrc(l, first):
        return x_in if first else xT

    def program():
        ws.n = 0
        cload = []
        cload.append(dma("sp", ident_f[:, :], cst_f[:, 0:128], [], ["ident_f"], "c0"))
        dma("sp", ones_f[:, :], cst_f[:, 128:256], [], ["ones_f"], "c0")
        dma("sp", sel[0:4, :], cst_f[0:4, 256:320], [], ["sel"], "c0")
        dma("sp", jtab[:, :], cst_f[:, 320:324], [], ["jtab"], "c0")
        dma("sp", ident_b[:, :], cst_b[:, 0:128], [], ["ident_b"], "c0")
        dma("sp", mask_b2[0][:, :], cst_b[:, 128:256], [], ["mask0"], "c0")
        dma("sp", mask_b2[1][:, :], cst_b[:, 256:384], [], ["mask1"], "c0")
        dma("sp", vecs[:, :], vec_fm[:, :], [], ["vecs"], "c0")
        dma("sp", kapbc[:, :], kap_in.partition_broadcast(128), [], ["kapbc"], "c0")
        dma("sp", gb4[0:4, :], gb_in[:, :], [], ["gb4"], "c0")
        dma("sp", m04[0:4, :], m0_in[:, :], [], ["m04"], "c0")
        dma("sp", rope_t[:, :], rope_in[:, :], [], ["rope"], "c0")
        P.add("dve", lambda e: e.memset(epsT[:, :], EPS), w=["eps"])
        ts(gb4[0:4, 16:32], gb4[0:4, 0:16], -1.0, None, ALU.mult, None, r=["gb4"], w=["ngb4"])
        act(csil[:, :], vecs[:, 144:160], AF.Silu, r=["vecs"], w=["csil"])
        cp("dve", sT[:, :], csil[:, :], r=["csil"], w=["sT"])

        P.scope = "mod"
        psm = PS[7]

        def mod_gen(l):
            wv = w_mod[l].rearrange("(k p) n -> p k n", p=128)
            for jt in range(48):
                sc_save = P.scope
                P.scope = "mod"
                slot, wk_ = ws.get(wtile_kc(wv[:, :, jt * 256:(jt + 1) * 256], 16, 256))
                wt = wview(slot, 16, 256)
                for cg in range(2):
                    j = jt * 2 + cg
                    for kc in range(16):
                        mm(psm[:, l * 96 + j:l * 96 + j + 1], wt[:, kc, cg * 128:(cg + 1) * 128], sT[:, kc:kc + 1],
                           kc == 0, kc == 15, r=[wk_, "sT"], w=[psk(7)])
                if jt == 47:
                    tt(modv[:, l * 96:(l + 1) * 96], psm[:, l * 96:(l + 1) * 96],
                       vecs[:, 160 + l * 96:160 + (l + 1) * 96], ALU.add, r=[psk(7), "vecs"], w=[("modv", l)])
                    mA_ = modA[:, l * 96:(l + 1) * 96]
                    mv_ = modv[:, l * 96:(l + 1) * 96]
                    stt(mA_[:, 0:16], mv_[:, 16:32], 1.0, vecs[:, l * 16:(l + 1) * 16], ALU.add, ALU.mult,
                        r=[("modv", l), "vecs"], w=[("modA", l)])
                    stt(mA_[:, 48:64], mv_[:, 64:80], 1.0, vecs[:, 64 + l * 16:64 + (l + 1) * 16], ALU.add, ALU.mult,
                        r=[("modv", l), "vecs"], w=[("modA", l)])
                P.scope = sc_save
                yield

        for _ in mod_gen(0):
            pass
        barrier()

        def norm_phase(src, Acol, Bcol, dst, rkeys_fn, l):
            P.add("dve", lambda e: e.memset(ones_b[:, :], 1.0), w=["ones_b"])
            for tt_ in range(4):
                tsl = slice(tt_ * 512, (tt_ + 1) * 512)
                dma("sp", xtile[:, :, :], src.rearrange("c p t -> p c t")[:, :, tsl],
                    r=[("xT", dc, tt_) for dc in range(16)], w=["xtile"], key="xt")
                psq = PS[tt_ % 2]
                for dc in range(16):
                    sq = sqt[dc % 2]
                    act(sq[:, :], xtile[:, dc, :], AF.Square, r=["xtile"], w=[("sq", dc % 2)])
                    mm(psq[:, :], ones_b[:, :], sq[:, :], dc == 0, dc == 15, r=[("sq", dc % 2), "ones_b"],
                       w=[psk(tt_ % 2)])
                act(rbt[:, :], psq[:, :], AF.Sqrt, r=[psk(tt_ % 2), "eps"], w=["rbt"], bias=epsT[:, 0:1], scale=1.0 / D)
                P.add("dve", lambda e: e.reciprocal(out=rbt[:, :], in_=rbt[:, :]), r=["rbt"], w=["rbt"])
                for dc in range(16):
                    tm = tmpn[dc % 2]
                    tt(tm[:, :], xtile[:, dc, :], rbt[:, :], ALU.mult, r=["xtile", "rbt"], w=[("tmpn", dc % 2)])
                    act(dst[:, dc, tsl], tm[:, :], AF.Identity, r=[("tmpn", dc % 2), ("modA", l)], w=[("hT", tt_)],
                        scale=Acol[:, dc:dc + 1], bias=Bcol[:, dc:dc + 1])

        def rmw_x(psb, dc, tt_, gcol, src, xp, xpk, bank):
            tsl = slice(tt_ * 512, (tt_ + 1) * 512)
            dma("sp", xp[:, :], src[dc][:, tsl], r=[("xT", dc, tt_)], w=[xpk], key=("L",) + xpk)
            stt(xp[:, :], psb[:, :], gcol, xp[:, :], ALU.mult, ALU.add, r=[psk(bank), xpk], w=[xpk])
            dma("act", xT[dc][:, tsl], xp[:, :], r=[xpk], w=[("xT", dc, tt_)], key=("S",) + xpk)

        for l in range(nl):
            first = (l == 0)
            mA = modA[:, l * 96:(l + 1) * 96]
            mv = modv[:, l * 96:(l + 1) * 96]
            win = w_in[l].rearrange("(k p) n -> p k n", p=128)
            mgen = mod_gen(l + 1) if l + 1 < nl else None

            def pump(mgen=mgen):
                if mgen is not None:
                    next(mgen, None)

            P.scope = "norm1"
            norm_phase(x_in if first else xT, mA[:, 0:16], mv[:, 0:16], hT, None, l)
            barrier()
            P.scope = "rtab"
            r8 = slice(l * 8, (l + 1) * 8)
            dma("sp", lgb[:, :], rdec_in[:, r8].partition_broadcast(128), [], ["lgb"], "k_lgb")
            act(lgb[:, :], lgb[:, :], AF.Exp, r=["lgb"], w=["lgb"])
            ts(lgb[:, :], lgb[:, :], -1.0, None, ALU.mult, None, r=["lgb"], w=["lgb"])
            for d in range(2):
                d4 = slice(d * 4, d * 4 + 4)
                ts(rs_row[:, d4], lgb[:, d4], jtab[:, d:d + 1], None, ALU.mult, None, r=["lgb", "jtab"], w=["rs_row"])
                ts(rs_wst[:, d4], lgb[:, d4], jtab[:, 2 + d:3 + d], None, ALU.mult, None, r=["lgb", "jtab"], w=["rs_wst"])
            act(rs_wsc[:, :], rs_row[:, :], AF.Exp, r=["rs_row"], w=["rs_wsc"], scale=-1.0)
            act(rs_row[:, :], rs_row[:, :], AF.Exp, r=["rs_row", "rs_wsc"], w=["rs_row"])
            act(rs_wst[:, :], rs_wst[:, :], AF.Exp, r=["rs_wst"], w=["rs_wst"])
            act(rs_a128[:, :], lgb[:, :], AF.Exp, r=["lgb"], w=["rs_a128"], scale=128.0)
            for d in range(2):
                for h in range(4):
                    i8 = d * 4 + h
                    ts(rs_a128k[:, i8 * 16:(i8 + 1) * 16], kapbc[:, d * 16:(d + 1) * 16], rs_a128[:, i8:i8 + 1], None,
                       ALU.mult, None, r=["kapbc", "rs_a128"], w=["rs_a128k"])

            for grp in range(2):
                if grp == 0:
                    P.scope = "gates"
                    dma("sp", keepT[:, :], keep_in[:, :], [], ["keepT"], "k_keep")
                    gsrc = win[:, :, 8192:8208]
                    slot, wk_ = ws.get([(lambda sl: sl[:, 0:256].rearrange("p (k n) -> p k n", n=16), gsrc)])
                    wg = slot[:, 0:256].rearrange("p (k n) -> p k n", n=16)
                    for d in range(2):
                        for ty in range(2):
                            tyi = d * 2 + ty
                            for tt_ in range(4):
                                tsl = slice(tt_ * 512, (tt_ + 1) * 512)
                                pb = (tyi * 4 + tt_) % 2
                                for kc in range(16):
                                    mm(PS[pb][0:4, :], wg[:, kc, tyi * 4:tyi * 4 + 4], hT[:, kc, tsl], kc == 0, kc == 15,
                                       r=[wk_, ("hT", tt_)], w=[psk(pb)])
                                bcol = l * 4 + tyi
                                if ty == 0:
                                    ts(GI[:, tsl], PS[pb][0:4, :], gb4[0:4, bcol:bcol + 1], None, ALU.add, None,
                                       r=[psk(pb), "gb4"], w=["GI"])
                                else:
                                    act(GE[:, tsl], PS[pb][0:4, :], AF.Exp, r=[psk(pb), "ngb4"], w=["GE"],
                                        scale=-1.0, bias=gb4[0:4, 16 + bcol:17 + bcol])
                        act(GE[:, :], GE[:, :], AF.Ln, r=["GE"], w=["GE"], bias=1.0)
                        P.add("dve", lambda e: e.tensor_tensor_scan(out=GB[:, :], data0=keepT[:, :], data1=GE[:, :],
                                                                     initial=0.0, op0=ALU.mult, op1=ALU.add),
                              r=["keepT", "GE"], w=["GB"])
                        GI3 = GI.rearrange("p (c t) -> p c t", t=128)
                        GE3 = GE.rearrange("p (c t) -> p c t", t=128)
                        GB3 = GB.rearrange("p (c t) -> p c t", t=128)
                        cp("dve", g_Bl[0:4, :].rearrange("p (c o) -> p c o", o=1), GB3[:, :, 127:128], r=["GB"], w=["gBl"])
                        if d == 0:
                            BP = GB
                            BP3 = GB3
                            bpk = "GB"
                        else:
                            tt(GE[:, :], GE[:, :], GB[:, :], ALU.subtract, r=["GE", "GB"], w=["GE"])
                            tt(GE3, GE3, GB3[:, :, 127:128].to_broadcast([4, 16, 128]), ALU.add, r=["GE", "GB"], w=["GE"])
                            BP = GE
                            BP3 = GE3
                            bpk = "GE"
                        tt(GI[:, :], GI[:, :], BP[:, :], ALU.add, r=["GI", bpk], w=["GI"])
                        P.add("dve", lambda e: e.tensor_reduce(out=g_umax[0:4, :], in_=GI3, axis=AX.X, op=ALU.max),
                              r=["GI"], w=["gumax"])
                        order = list(range(16)) if d == 0 else list(range(15, -1, -1))
                        prev = None
                        for c in order:
                            if prev is None:
                                cp("dve", g_mk[0:4, c:c + 1], m04[0:4, l * 2 + d:l * 2 + d + 1], r=["m04"], w=["gmk"])
                            else:
                                tt(g_mk[0:4, c:c + 1], g_m[0:4, prev:prev + 1], kapbc[0:4, d * 16 + c:d * 16 + c + 1],
                                   ALU.mult, r=["gm", "kapbc"], w=["gmk"])
                            tt(g_M[0:4, c:c + 1], g_mk[0:4, c:c + 1], g_umax[0:4, c:c + 1], ALU.max,
                               r=["gmk", "gumax"], w=["gM"])
                            tt(g_m[0:4, c:c + 1], g_M[0:4, c:c + 1], g_Bl[0:4, c:c + 1], ALU.subtract,
                               r=["gM", "gBl"], w=["gm"])
                            prev = c
                        tt(g_ast[0:4, :], g_mk[0:4, :], g_M[0:4, :], ALU.subtract, r=["gmk", "gM"], w=["gast"])
                        act(g_ast[0:4, :], g_ast[0:4, :], AF.Exp, r=["gast"], w=["gast"])
                        tt(g_ast[0:4, :], g_ast[0:4, :], kapbc[0:4, d * 16:(d + 1) * 16], ALU.mult,
                           r=["gast", "kapbc"], w=["gast"])
                        Mb = g_M[0:4, :].rearrange("p (c o) -> p c o", o=1).to_broadcast([4, 16, 128])
                        tt(GI3, GI3, Mb, ALU.subtract, r=["GI", "gM"], w=["GI"])
                        act(GI[:, :], GI[:, :], AF.Exp, r=["GI"], w=["GI"])
                        tt(BP3, BP3, Mb, ALU.subtract, r=[bpk, "gM"], w=[bpk])
                        act(BP[:, :], BP[:, :], AF.Exp, r=[bpk], w=[bpk])
                        for q, (src_, sk) in enumerate(((GI, "GI"), (BP, bpk))):
                            for c in range(16):
                                o0 = q * 64 + c * 4
                                mm(PS[2][:, o0:o0 + 4], src_[0:4, c * 128:(c + 1) * 128], ident_f[0:4, 0:4], True, True,
                                   r=[sk, "ident_f"], w=[psk(2)])
                        cp("dve", gtok[:, d * 128:(d + 1) * 128], PS[2][:, 0:128], r=[psk(2)], w=[("gtok", d)])
                        tt(g_a4[0:4, :].rearrange("p (h c) -> p h c", c=16), sel[0:4, :].rearrange("p (h c) -> p h c", c=16),
                           g_ast[0:4, :].rearrange("p (o c) -> p o c", o=1).to_broadcast([4, 4, 16]), ALU.mult,
                           r=["sel", "gast"], w=["ga4"])
                        mm(PS[3][:, 0:64], ones_f[0:4, :], g_a4[0:4, :], True, True, r=["ga4", "ones_f"], w=[psk(3)])
                        cp("dve", gbc[:, d * 64:(d + 1) * 64], PS[3][:, 0:64], r=[psk(3)], w=[("gbc", d)])
                        g_m3 = g_m[0:4, :].rearrange("p (s two) -> p s two", two=2)
                        dma("sp", oM[:, l, d, :], g_m3[:, :, 1 - d], r=["gm"], w=[("oM", l, d)], key="om",
                            allow_slow_non_contiguous=True)
                    barrier()

                for hh in range(4):
                    base = grp * 4096
                    cols = [base + i * 1024 + hh * 256 for i in range(4)]
                    gsrc = (mng if grp == 0 else rng_)[l:l + 1, hh * 256:(hh + 1) * 256]
                    dma("sp", Gn[:, :], gsrc.partition_broadcast(128), [], ["Gn"], "k_gn")
                    P.add("dve", lambda e: e.memset(vaug[:, :, 256:258], 1.0), w=["vaug1"])
                    P.scope = "proj%d" % grp
                    pend = []
                    gidx = [0]
                    tcnt = [0]

                    def flush(upto):
                        while pend and (upto is None or pend[0][0] <= upto):
                            pend.pop(0)[1]()

                    def emit_tr(j, wi):
                        dstT = qT if wi == 0 else kT
                        dkey = "qT" if wi == 0 else "kT"
                        jsl = slice(j * 128, (j + 1) * 128)
                        pb = 4 + tcnt[0] % 3
                        tcnt[0] += 1
                        for dc in range(2):
                            P.add("pe", lambda e, pb=pb, dc=dc, j=j: e.transpose(
                                PS[pb][:, dc * 128:(dc + 1) * 128], hsum[:, j, dc * 128:(dc + 1) * 128], ident_f[:, :]),
                                r=[("hsum", j), "ident_f"], w=[psk(pb)])
                        cp("dve", dstT[:, :, jsl], PS[pb][:, 0:256].rearrange("p (c t) -> p c t", t=128),
                           r=[psk(pb)], w=[dkey])

                    LAG = 2 if grp == 0 else 4
                    for wi in range(4):
                        slot, wk_ = ws.get(wtile_kc(win[:, :, cols[wi]:cols[wi] + 256], 16, 256))
                        wt = wview(slot, 16, 256)
                        for c in range(16):
                            csl = slice(c * 128, (c + 1) * 128)
                            pb = c % 4
                            for kc in range(16):
                                mm(PS[pb][:, 0:256], hT[:, kc, csl], wt[:, kc, :], kc == 0, kc == 15,
                                   r=[wk_, ("hT", c // 4)], w=[psk(pb)])
                            if wi < 2:
                                sc = 1.0
                                if (grp == 0 and wi == 1) or (grp == 1 and wi == 0):
                                    sc = 1.0 / 16.0
                                act(hsum[:, c, :], PS[pb][:, 0:256], AF.Copy, r=[psk(pb)], w=[("hsum", c)], scale=sc)
                                ready = []
                                if grp == 0:
                                    ready = [c]
                                elif c % 4 == 3:
                                    g4 = slice(c - 3, c + 1)
                                    gk = [("hsum", j) for j in range(c - 3, c + 1)]
                                    for half in range(2):
                                        x1 = hsum[:, g4, half * 128:half * 128 + 64]
                                        x2 = hsum[:, g4, half * 128 + 64:half * 128 + 128]
                                        if half == 0:
                                            cs = rope_t[:, 0:1024].rearrange("p (c e) -> p c e", e=64)[:, g4, :]
                                            sn = rope_t[:, 1024:2048].rearrange("p (c e) -> p c e", e=64)[:, g4, :]
                                        else:
                                            cs = rope_t[:, 2048:2112].rearrange("p (o e) -> p o e", o=1).to_broadcast([128, 4, 64])
                                            sn = rope_t[:, 2112:2176].rearrange("p (o e) -> p o e", o=1).to_broadcast([128, 4, 64])
                                        rA = ropeA[:, 0:4, :]
                                        rB = ropeB[:, 0:4, :]
                                        tt(rA, x2, sn, ALU.mult, r=gk + ["rope"], w=["og"])
                                        tt(rB, x1, sn, ALU.mult, r=gk + ["rope", "og"], w=["og"])
                                        tt(x1, x1, cs, ALU.mult, r=gk + ["rope", "og"], w=gk)
                                        tt(x1, x1, rA, ALU.subtract, r=gk + ["og"], w=gk)
                                        tt(x2, x2, cs, ALU.mult, r=gk + ["rope"], w=gk)
                                        tt(x2, x2, rB, ALU.add, r=gk + ["og"], w=gk)
                                    ready = list(range(c - 3, c + 1))
                                if ready:
                                    if wi == 1:
                                        rs_ = slice(ready[0], ready[-1] + 1)
                                        cp("act", ktok[:, rs_, :], hsum[:, rs_, :], r=[("hsum", j) for j in ready], w=["ktok"])
                                    for j in ready:
                                        pend.append((gidx[0] + LAG, lambda j=j, wi=wi: emit_tr(j, wi)))
                            elif wi == 2:
                                cp("dve", vaug[:, c, 0:256], PS[pb][:, 0:256], r=[psk(pb)], w=[("vaug", c)])
                            else:
                                act(og[:, c, :], PS[pb][:, 0:256], AF.Sigmoid if grp == 0 else AF.Silu,
                                    r=[psk(pb)], w=["og"])
                            gidx[0] += 1
                            flush(gidx[0])
                    flush(None)
                    P.scope = "scan%d" % grp
                    cur = [0, 0]
                    for d in range(2):
                        src = (initC if grp == 0 else initS)[l, d, hh].rearrange("(c p) e -> p c e", p=128)
                        ne = 257 if grp == 0 else 256
                        dma("sp", Cst[d][0][:, :, 0:ne], src, [], [("Cst", d, 0)], ("ci", d))
                    written = [False] * 16
                    for step in range(16):
                        for d in range(2):
                            c = step if d == 0 else 15 - step
                            csl = slice(c * 128, (c + 1) * 128)
                            pp = cur[d]
                            ne = 257 if grp == 0 else 256
                            if grp == 0:
                                gt = gtok[:, d * 128:(d + 1) * 128]
                                wsc = gt[:, c * 4 + hh:c * 4 + hh + 1]
                                wst = wsc
                                flo = gt[:, 64 + c * 4 + hh:64 + c * 4 + hh + 1]
                                a_in = gbc[:, d * 64 + hh * 16 + c:d * 64 + hh * 16 + c + 1]
                                a_st = a_in
                                skeys = [("gtok", d), ("gbc", d)]
                            else:
                                i8 = d * 4 + hh
                                wsc = rs_wsc[:, i8:i8 + 1]
                                wst = rs_wst[:, i8:i8 + 1]
                                row = rs_row[:, i8:i8 + 1]
                                a_in = kapbc[:, d * 16 + c:d * 16 + c + 1]
                                a_st = rs_a128k[:, i8 * 16 + c:i8 * 16 + c + 1]
                                skeys = ["rs_wsc", "rs_wst", "rs_row", "kapbc", "rs_a128k"]
                            act(Cbf[d][:, :, 0:ne], Cst[d][pp][:, :, 0:ne], AF.Copy, r=[("Cst", d, pp)] + skeys,
                                w=[("Cbf", d)], scale=a_in)
                            for dc in range(2):
                                mm(PS[d][:, 0:128], kT[:, dc, csl], qT[:, dc, csl], dc == 0, dc == 1,
                                   r=["kT", "qT"], w=[psk(d)])
                            stt(STb[d][:, :], PS[d][:, 0:128], wsc, mask_b2[d][:, :], ALU.mult, ALU.mult,
                                r=[psk(d), "mask%d" % d] + skeys, w=[("STb", d)])
                            for dc in range(2):
                                mm(PS[2 + d][:, 0:ne], qT[:, dc, csl], Cbf[d][:, dc, 0:ne], dc == 0, False,
                                   r=["qT", ("Cbf", d)], w=[psk(2 + d)])
                            mm(PS[2 + d][:, 0:ne], STb[d][:, :], vaug[:, c, 0:ne], False, True,
                               r=[("STb", d), ("vaug", c), "vaug1"], w=[psk(2 + d)])
                            if grp == 0:
                                den = small[:, d * 2:d * 2 + 1]
                                act(den, PS[2 + d][:, 256:257], AF.Abs, r=[psk(2 + d)], w=[("den", d)])
                                tt(den, den, flo, ALU.max, r=[("den", d)] + skeys, w=[("den", d)])
                                P.add("dve", lambda e, den=den: e.reciprocal(out=den, in_=den), r=[("den", d)], w=[("den", d)])
                                rsc = den
                                rk = [("den", d)]
                            else:
                                rsc = row
                                rk = skeys
                            if not written[c]:
                                act(hsum[:, c, :], PS[2 + d][:, 0:256], AF.Copy, r=[psk(2 + d)] + rk, w=[("hsum", c)],
                                    scale=rsc)
                                written[c] = True
                                fin = False
                            else:
                                stt(hsum[:, c, :], PS[2 + d][:, 0:256], rsc, hsum[:, c, :], ALU.mult, ALU.add,
                                    r=[psk(2 + d), ("hsum", c)] + rk, w=[("hsum", c)])
                                fin = True
                            ts(wkb[d][:, :], ktok[:, c, :], wst, None, ALU.mult, None, r=["ktok"] + skeys, w=[("wk", d)],
                               eng="pool")
                            for dc in range(2):
                                mm(PS[4 + dc][:, 0:ne], wkb[d][:, dc * 128:(dc + 1) * 128], vaug[:, c, 0:ne], True, True,
                                   r=[("wk", d), ("vaug", c), "vaug1"], w=[psk(4 + dc)])
                            for dc in range(2):
                                stt(Cst[d][1 - pp][:, dc, 0:ne], Cst[d][pp][:, dc, 0:ne], a_st, PS[4 + dc][:, 0:ne],
                                    ALU.mult, ALU.add, r=[("Cst", d, pp), psk(4 + dc)] + skeys, w=[("Cst", d, 1 - pp)])
                            cur[d] = 1 - pp
                            if (d == 0 and c % 2 == 1) or (d == 1 and c % 2 == 0):
                                sq_ = c // 2
                                if grp == 0:
                                    dst = oC[sq_, l, d, hh].rearrange("(c p) e -> p c e", p=128)
                                else:
                                    dst = oS[sq_, l, d, hh].rearrange("(c p) e -> p c e", p=128)
                                dma("sp", dst, Cst[d][1 - pp][:, :, 0:ne], r=[("Cst", d, 1 - pp)],
                                    w=[("ost", grp, l, d, hh, sq_)], key=("so", d, 1 - pp))
                            if fin:
                                i2 = c % 2
                                ssq = small[:, 4 + i2:5 + i2]
                                act(t1[i2][:, :], hsum[:, c, :], AF.Square, r=[("hsum", c)], w=[("t1", i2), ("ssq", i2)],
                                    accum_out=ssq)
                                act(ssq, ssq, AF.Sqrt, r=[("ssq", i2), "eps"], w=[("ssq", i2)], bias=epsT[:, 0:1],
                                    scale=1.0 / 256.0)
                                P.add("dve", lambda e, ssq=ssq: e.reciprocal(out=ssq, in_=ssq), r=[("ssq", i2)], w=[("ssq", i2)])
                                stt(t1[i2][:, :], hsum[:, c, :], ssq, og[:, c, :], ALU.mult, ALU.mult,
                                    r=[("hsum", c), ("ssq", i2), "og"], w=[("t1", i2)])
                                tt(ytok[i2][:, :], t1[i2][:, :], Gn[:, :], ALU.mult, r=[("t1", i2), "Gn"], w=[("ytok", i2)])
                                psy = PS[6][:, i2 * 128:(i2 + 1) * 128].bitcast(BF16).rearrange("p (c t) -> p c t", t=128)
                                for dc in range(2):
                                    P.add("pe", lambda e, psy=psy, dc=dc, i2=i2: e.transpose(
                                        psy[:, dc, :], ytok[i2][:, dc * 128:(dc + 1) * 128], ident_b[:, :]),
                                        r=[("ytok", i2), "ident_b"], w=[psk(6)])
                                cp("act", yT[:, hh * 2:hh * 2 + 2, csl], psy, r=[psk(6)], w=["yT"])
                        if step % 3 == 2 or step == 0:
                            pump()
                barrier()
                P.scope = "outproj"
                wo = w_out[l, grp * 1024:(grp + 1) * 1024, :].rearrange("(k p) n -> p k n", p=128)
                n = 0
                for jt in range(8):
                    slot, wk_ = ws.get(wtile_kc(wo[:, :, jt * 256:(jt + 1) * 256], 8, 256))
                    wt = wview(slot, 8, 256)
                    for cg in range(2):
                        dc_ = jt * 2 + cg
                        for tt_ in range(4):
                            tsl = slice(tt_ * 512, (tt_ + 1) * 512)
                            pb = n % 7
                            for kc in range(8):
                                mm(PS[pb][:, :], wt[:, kc, cg * 128:(cg + 1) * 128], yT[:, kc, tsl], kc == 0, kc == 7,
                                   r=[wk_, "yT"], w=[psk(pb)])
                            rmw_x(PS[pb], dc_, tt_, mv[:, 32 + dc_:33 + dc_], x_in if (first and grp == 0) else xT,
                                  xp_o[n % NXO], ("xpo", n % NXO), pb)
                            n += 1
                barrier()

            if mgen is not None:
                for _ in mgen:
                    pass
            P.scope = "norm2"
            norm_phase(xT, mA[:, 48:64], mv[:, 48:64], h2T, None, l)
            barrier()
            wgu = w_gu[l].rearrange("(k p) n -> p k n", p=128)
            for half in range(2):
                n = 0
                P.scope = "ffn_gu"
                for hc in range(22):
                    hg = half * 22 + hc
                    parts = [
                        (lambda sl: sl[:, 0:4096].rearrange("p (k n) -> p k n", n=256)[:, :, 0:128],
                         wgu[:, :, hg * 128:(hg + 1) * 128]),
                        (lambda sl: sl[:, 0:4096].rearrange("p (k n) -> p k n", n=256)[:, :, 128:256],
                         wgu[:, :, DFF + hg * 128:DFF + (hg + 1) * 128]),
                    ]
                    slot, wk_ = ws.get(parts)
                    wt = wview(slot, 16, 256)
                    for tt_ in range(4):
                        tsl = slice(tt_ * 512, (tt_ + 1) * 512)
                        pa = (2 * n) % 6
                        pu = (2 * n + 1) % 6
                        for kc in range(16):
                            mm(PS[pa][:, :], wt[:, kc, 0:128], h2T[:, kc, tsl], kc == 0, kc == 15,
                               r=[wk_, ("hT", tt_)], w=[psk(pa)])
                        for kc in range(16):
                            mm(PS[pu][:, :], wt[:, kc, 128:256], h2T[:, kc, tsl], kc == 0, kc == 15,
                               r=[wk_, ("hT", tt_)], w=[psk(pu)])
                        sa = sat[n % 2]
                        act(sa[:, :], PS[pa][:, :], AF.Silu, r=[psk(pa)], w=[("sat", n % 2)])
                        tt(actT[:, hc, tsl], sa[:, :], PS[pu][:, :], ALU.mult, r=[("sat", n % 2), psk(pu)], w=["actT"])
                        n += 1
                P.scope = "ffn_down"
                wdn = w_down[l, half * 22 * 128:(half + 1) * 22 * 128, :].rearrange("(k p) n -> p k n", p=128)
                n = 0
                for dc_ in range(16):
                    slot, wk_ = ws.get(wtile_kc(wdn[:, :, dc_ * 128:(dc_ + 1) * 128], 22, 128))
                    wt = wview(slot, 22, 128)
                    for tt_ in range(4):
                        tsl = slice(tt_ * 512, (tt_ + 1) * 512)
                        pb = n % 6
                        for kc in range(22):
                            mm(PS[pb][:, :], wt[:, kc, :], actT[:, kc, tsl], kc == 0, kc == 21,
                               r=[wk_, "actT"], w=[psk(pb)])
                        rmw_x(PS[pb], dc_, tt_, mv[:, 80 + dc_:81 + dc_], xT, xp_f[n % NXF], ("xpf", n % NXF), pb)
                        n += 1
            barrier()

        P.scope = "final"
        fg = vecs[:, 128:144]
        srcx = xT if nl > 0 else x_in
        P.add("dve", lambda e: e.memset(ones_b[:, :], 1.0), w=["ones_b"])
        for tt_ in range(4):
            tsl = slice(tt_ * 512, (tt_ + 1) * 512)
            dma("sp", xtile[:, :, :], srcx.rearrange("c p t -> p c t")[:, :, tsl], r=[("xT", dc, tt_) for dc in range(16)], w=["xtile"], key="xt")
            psq = PS[tt_ % 2]
            for dc in range(16):
                sq = sqt[dc % 2]
                act(sq[:, :], xtile[:, dc, :], AF.Square, r=["xtile"], w=[("sq", dc % 2)])
                mm(psq[:, :], ones_b[:, :], sq[:, :], dc == 0, dc == 15, r=[("sq", dc % 2), "ones_b"], w=[psk(tt_ % 2)])
            act(rbt[:, :], psq[:, :], AF.Sqrt, r=[psk(tt_ % 2), "eps"], w=["rbt"], bias=epsT[:, 0:1], scale=1.0 / D)
            P.add("dve", lambda e: e.reciprocal(out=rbt[:, :], in_=rbt[:, :]), r=["rbt"], w=["rbt"])
            for dc in range(16):
                stt(xtile[:, dc, :], xtile[:, dc, :], fg[:, dc:dc + 1], rbt[:, :], ALU.mult, ALU.mult,
                    r=["xtile", "rbt", "vecs"], w=["xtile"])
            dma("sp", yT_out.rearrange("c p t -> p c t")[:, :, tsl], xtile[:, :, :], r=["xtile"], w=[("yout", tt_)], key="xt")

    P.dry = True
    program()
    P.dry = False
    program()
    P.emit(nc)
    return nc


_NC_CACHE = {}


def _consts():
    cf = np.zeros((128, 1536), np.float32)
    cf[:, 0:128] = np.eye(128, dtype=np.float32)
    cf[:, 128:256] = 1.0
    selm = np.zeros((4, 4, 16), np.float32)
    for h in range(4):
        selm[h, h, :] = 1.0
    cf[0:4, 256:320] = selm.reshape(4, 64)
    t = np.arange(128, dtype=np.float32)
    cf[:, 320] = t + 1.0
    cf[:, 321] = 128.0 - t
    cf[:, 322] = 127.0 - t
    cf[:, 323] = t
    cb = np.zeros((128, 384), np.float32)
    cb[:, 0:128] = np.eye(128)
    s = np.arange(128)[:, None]
    tt = np.arange(128)[None, :]
    cb[:, 128:256] = (tt >= s)
    cb[:, 256:384] = (tt <= s)
    keep = np.ones((4, T), np.float32)
    keep[:, ::128] = 0.0
    return cf, cb.astype(ml_dtypes.bfloat16), keep


def _rope(latent):
    out = np.zeros((128, 2 * 1024 + 128), np.float32)
    if not latent:
        out[:, 0:1024] = 1.0
        out[:, 2048:2112] = 1.0
        return out
    inv = (10000.0 ** (-np.arange(64, dtype=np.float32) / 64.0)).astype(np.float32)
    p = np.arange(128)
    c = np.arange(16)
    r = (2 * c[None, :] + (p[:, None] // 64)).astype(np.float32)
    col = (p % 64).astype(np.float32)
    ang_r = (r[:, :, None] * inv[None, None, :]).astype(np.float32)
    ang_c = (col[:, None] * inv[None, :]).astype(np.float32)
    out[:, 0:1024] = np.cos(ang_r).reshape(128, 1024)
    out[:, 1024:2048] = np.sin(ang_r).reshape(128, 1024)
    out[:, 2048:2112] = np.cos(ang_c)
    out[:, 2112:2176] = np.sin(ang_c)
    return out


def kernel(x_prompt, x_sample, c, state_mlstm_C, state_mlstm_n, state_mlstm_m, state_ret_S,
           c_ctx, w_mod, b_mod, norm1_g, w_in, mlstm_gate_b, ret_decay, mlstm_norm_g, ret_norm_g,
           w_out, norm2_g, w_gu, w_down, final_norm_g, _nl=NL, _cores=None, _trace=False, _scopes=False):
    f = lambda a: np.ascontiguousarray(np.asarray(a, dtype=np.float32))
    x_prompt, x_sample, c, c_ctx = f(x_prompt), f(x_sample), f(c), f(c_ctx)
    nl = _nl
    if (nl, _scopes) not in _NC_CACHE:
        _NC_CACHE[(nl, _scopes)] = build(nl, _scopes)
    nc = _NC_CACHE[(nl, _scopes)]
    cf, cb, keep = _consts()

    def fm(v):
        v = np.asarray(v, np.float32)
        return v.reshape(v.shape[:-1] + (16, 128))

    shared = {
        "w_mod": f(w_mod), "w_in": f(w_in), "w_out": f(w_out), "w_gu": f(w_gu), "w_down": f(w_down),
        "mnorm_g": f(mlstm_norm_g), "rnorm_g": f(ret_norm_g), "cst_f": cf, "cst_b": cb, "keep": keep,
        "rdec": f(ret_decay).reshape(1, 32),
    }
    gbh = np.zeros((4, 32), np.float32)
    gbh[:, 0:16] = np.transpose(f(mlstm_gate_b), (2, 0, 1)).reshape(4, 16)
    shared["gb"] = gbh
    n1 = np.transpose(fm(norm1_g), (2, 0, 1)).reshape(128, 64)
    n2 = np.transpose(fm(norm2_g), (2, 0, 1)).reshape(128, 64)
    fgm = fm(final_norm_g).T
    bm = np.transpose(f(b_mod).reshape(4, 96, 128), (2, 0, 1)).reshape(128, 384)
    rope_lat, rope_ctx = _rope(True), _rope(False)
    in_maps = []
    for core in range(8):
        m = dict(shared)
        latent = core < 4
        if latent:
            b = core
            xs = x_sample[b]
            cv = c[b]
            Caug = np.concatenate([f(state_mlstm_C)[b], f(state_mlstm_n)[b][..., None]], axis=-1)
            S0 = f(state_ret_S)[b]
            m0 = f(state_mlstm_m)[b]
            kap = np.ones((1, 32), np.float32)
        else:
            j = core - 4
            xs = x_prompt[8 * j:8 * j + 8].reshape(T, D)
            cv = c_ctx
            Caug = np.zeros((NL, 2, 4, 256, 257), np.float32)
            S0 = np.zeros((NL, 2, 4, 256, 256), np.float32)
            m0 = np.zeros((NL, 2, 4), np.float32)
            kap = np.ones((1, 32), np.float32)
            kap[0, 0:16:2] = 0.0
            kap[0, 17:32:2] = 0.0
            kap[0, 0] = 1.0
            kap[0, 31] = 1.0
        m["x_in"] = np.ascontiguousarray(xs.T.reshape(16, 128, T))
        vec = np.zeros((128, 544), np.float32)
        vec[:, 0:64] = n1
        vec[:, 64:128] = n2
        vec[:, 128:144] = fgm
        vec[:, 144:160] = cv.reshape(16, 128).T
        vec[:, 160:544] = bm
        m["vec_fm"] = vec
        m["m0"] = np.ascontiguousarray(np.transpose(m0, (2, 0, 1)).reshape(4, 8))
        m["kap"] = kap
        m["init_Caug"] = np.ascontiguousarray(Caug)
        m["init_S"] = np.ascontiguousarray(S0)
        m["rope"] = rope_lat if latent else rope_ctx
        in_maps.append(m)
    if _cores is not None:
        res = run_bass_kernel_spmd(nc, [in_maps[i] for i in _cores], core_ids=list(range(len(_cores))), trace=_trace)
        return res
    res = run_bass_kernel_spmd(nc, in_maps, core_ids=list(range(8)))
    R = res.results
    y_sample = np.stack([R[b]["yT"].reshape(D, T).T for b in range(4)], axis=0)
    y_prompt = np.concatenate([R[4 + j]["yT"].reshape(D, T).T.reshape(8, 256, D) for j in range(4)], axis=0)
    oCa = np.concatenate([R[4 + j]["o_Caug"] for j in range(4)], axis=0)
    new_C = np.ascontiguousarray(oCa[..., :256])
    new_n = np.ascontiguousarray(oCa[..., 256])
    new_S = np.concatenate([R[4 + j]["o_S"] for j in range(4)], axis=0)
    new_m = np.concatenate([np.transpose(R[4 + j]["o_m"], (3, 1, 2, 0)) for j in range(4)], axis=0)
    return (y_prompt.astype(np.float32), y_sample.astype(np.float32), new_C.astype(np.float32),
            new_n.astype(np.float32), np.ascontiguousarray(new_m).astype(np.float32), new_S.astype(np.float32))
```

```python
import numpy as np
import ml_dtypes
import concourse.bass as bass
import concourse.mybir as mybir
from concourse.bass_utils import run_bass_kernel_spmd

F32 = mybir.dt.float32
BF16 = mybir.dt.bfloat16
AF = mybir.ActivationFunctionType
ALU = mybir.AluOpType
AX = mybir.AxisListType

D = 2048
T = 2048
NCH = 16
DFF = 5632
DIN = 8208
EPS = 1e-6
NL = 4
PH = "*PH*"


class Op:
    __slots__ = ("eng", "fn", "deps", "dma_key", "needs_inc", "inc_val", "scope")


class Prog:
    ENGS = ("pe", "act", "dve", "pool", "sp")

    def __init__(self):
        self.ops = {e: [] for e in self.ENGS}
        self.last_w = {}
        self.readers = {}
        self.dma_cnt = {}
        self.dry = False
        self.scope = None
        self.use_scopes = False

    def add(self, eng, fn, r=(), w=(), dma=None, phase=True):
        if self.dry:
            return None
        op = Op()
        op.eng = eng
        op.fn = fn
        op.dma_key = dma
        op.needs_inc = False
        op.inc_val = 0
        op.scope = self.scope
        deps = []
        r = list(r)
        w = list(w)
        if phase:
            r.append(PH)

        def dep(o):
            if o is None:
                return
            if o.dma_key is not None:
                deps.append((o, self.dma_cnt[o.dma_key]))
            else:
                if o.eng == "pe" and eng == "pe" and dma is None:
                    return
                o.needs_inc = True
                deps.append((o, None))

        for k in r:
            dep(self.last_w.get(k))
        for k in w:
            dep(self.last_w.get(k))
            for o in self.readers.get(k, ()):
                dep(o)
        if dma is not None:
            self.dma_cnt[dma] = self.dma_cnt.get(dma, 0) + 16
        for k in r:
            self.readers.setdefault(k, []).append(op)
        for k in w:
            self.last_w[k] = op
            self.readers[k] = []
        op.deps = deps
        self.ops[eng].append(op)
        return op

    def emit(self, nc):
        for e, ops in self.ops.items():
            n = 0
            for op in ops:
                if op.dma_key is None and op.needs_inc:
                    n += 1
                    op.inc_val = n
        sems = {}
        for e in self.ENGS:
            sems[("c", e)] = nc.alloc_semaphore("c_" + e)
        for i, k in enumerate(self.dma_cnt):
            sems[("d", k)] = nc.alloc_semaphore("d_%d" % i)
        prog = self

        def run(name, eng):
            seen = {}
            cur_scope = None
            for op in prog.ops[name]:
                if prog.use_scopes and op.scope != cur_scope:
                    if cur_scope is not None:
                        nc.pop_named_scope(cur_scope)
                    if op.scope is not None:
                        nc.push_named_scope(op.scope)
                    cur_scope = op.scope
                need = {}
                for (o, v) in op.deps:
                    if o.dma_key is not None:
                        key = ("d", o.dma_key)
                        val = v
                    else:
                        key = ("c", o.eng)
                        val = o.inc_val
                    if val > need.get(key, 0):
                        need[key] = val
                for key, val in need.items():
                    if seen.get(key, 0) >= val:
                        continue
                    eng.wait_ge(sems[key], val)
                    seen[key] = val
                ins = op.fn(eng)
                if op.dma_key is not None:
                    ins.then_inc(sems[("d", op.dma_key)], 16)
                elif op.needs_inc:
                    ins.then_inc(sems[("c", name)], 1)
            if prog.use_scopes and cur_scope is not None:
                nc.pop_named_scope(cur_scope)
            if name == "sp":
                for k, tot in prog.dma_cnt.items():
                    if seen.get(("d", k), 0) < tot:
                        eng.wait_ge(sems[("d", k)], tot)

        with nc.Block() as block:
            @block.tensor
            def _(e):
                run("pe", e)

            @block.scalar
            def _(e):
                run("act", e)

            @block.vector
            def _(e):
                run("dve", e)

            @block.gpsimd
            def _(e):
                run("pool", e)

            @block.sync
            def _(e):
                run("sp", e)


class Arena:
    def __init__(self, ar, nwords):
        self.ar = ar
        self.nbytes = nwords * 4

    def view(self, off, nelem, dtype, parts=128):
        assert off % 4 == 0
        nb = nelem * (4 if dtype == F32 else 2)
        nb4 = (nb + 3) // 4 * 4
        assert off + nb4 <= self.nbytes, (off, nb4, self.nbytes)
        ap = self.ar[0:parts, off // 4:(off + nb4) // 4]
        if dtype != F32:
            ap = ap.bitcast(dtype)
            if nb4 != nb:
                ap = ap[:, 0:nelem]
        return ap


def build(nl=NL, scopes=False):
    nc = bass.Bass("TRN2", target_bir_lowering=False)
    P = Prog()
    P.use_scopes = scopes

    def din(name, shape, dtype=F32):
        return nc.dram_tensor(name, shape, dtype, kind="ExternalInput").ap()

    def dout(name, shape, dtype=F32):
        return nc.dram_tensor(name, shape, dtype, kind="ExternalOutput").ap()

    x_in = din("x_in", [16, 128, T])
    vec_fm = din("vec_fm", [128, 544])
    gb_in = din("gb", [4, 32])
    m0_in = din("m0", [4, 8])
    rdec_in = din("rdec", [1, 32])
    kap_in = din("kap", [1, 32])
    initC = din("init_Caug", [NL, 2, 4, 256, 257])
    initS = din("init_S", [NL, 2, 4, 256, 256])
    w_mod = din("w_mod", [NL, D, 6 * D])
    w_in = din("w_in", [NL, D, DIN])
    w_out = din("w_out", [NL, D, D])
    w_gu = din("w_gu", [NL, D, 2 * DFF])
    w_down = din("w_down", [NL, DFF, D])
    mng = din("mnorm_g", [NL, 1024])
    rng_ = din("rnorm_g", [NL, 1024])
    cst_f = din("cst_f", [128, 1536])
    cst_b = din("cst_b", [128, 384], BF16)
    keep_in = din("keep", [4, T])
    rope_in = din("rope", [128, 2 * 16 * 64 + 2 * 64])

    yT_out = dout("yT", [16, 128, T])
    oC = dout("o_Caug", [8, NL, 2, 4, 256, 257])
    oS = dout("o_S", [8, NL, 2, 4, 256, 256])
    oM = dout("o_m", [4, NL, 2, 8])
    xT = nc.dram_tensor("xT_scr", [16, 128, T], F32, kind="Internal").ap()

    NW = 52900
    AR = nc.alloc_sbuf_tensor("arena", [128, NW], F32)
    A = Arena(AR, NW)
    off = [0]

    def alloc(nelem, dtype, parts=128):
        nb = (nelem * (4 if dtype == F32 else 2) + 31) // 32 * 32
        o = off[0]
        off[0] += nb
        return A.view(o, nelem, dtype, parts), o

    ident_b, _ = alloc(128, BF16)
    mask_b2 = [alloc(128, BF16)[0], alloc(128, BF16)[0]]
    ident_f, _ = alloc(128, F32)
    ones_f, _ = alloc(128, F32)
    vecs, _ = alloc(544, F32)
    modv, _ = alloc(4 * 96, F32)
    modA, _ = alloc(4 * 96, F32)
    sT, _ = alloc(16, BF16)
    csil, _ = alloc(16, F32)
    kapbc, _ = alloc(32, F32)
    gb4, _ = alloc(32, F32)
    m04, _ = alloc(8, F32)
    lgb, _ = alloc(8, F32)
    rs_row, _ = alloc(8, F32)
    rs_wsc, _ = alloc(8, F32)
    rs_wst, _ = alloc(8, F32)
    rs_a128, _ = alloc(8, F32)
    rs_a128k, _ = alloc(128, F32)
    jtab, _ = alloc(4, F32)
    gtok, _ = alloc(256, F32)
    gbc, _ = alloc(128, F32)
    g_umax, _ = alloc(16, F32)
    g_M, _ = alloc(16, F32)
    g_mk, _ = alloc(16, F32)
    g_m, _ = alloc(16, F32)
    g_Bl, _ = alloc(16, F32)
    g_ast, _ = alloc(16, F32)
    g_a4, _ = alloc(64, F32)
    sel, _ = alloc(64, F32)
    rope_t, _ = alloc(2 * 1024 + 128, F32)
    Gn, _ = alloc(256, F32)
    small, _ = alloc(16, F32)
    epsT, _ = alloc(1, F32)
    wslot = [alloc(16 * 256, BF16)[0], alloc(16 * 256, BF16)[0]]
    BIG = off[0]
    o_hT = BIG
    o_yT = o_hT + 65536
    o_head = o_yT + 32768
    o_qT = o_head
    o_kT = o_qT + 8192
    o_ktok = o_kT + 8192
    o_vaug = o_ktok + 8192
    o_og = o_vaug + 8256
    o_hsum = o_og + 8192
    o_scan = o_hsum + 16384
    o_Cst = o_scan
    o_Cbf = o_Cst + 8224
    o_STb = o_Cbf + 2064
    o_wk = o_STb + 512
    o_ytok = o_wk + 1024
    o_t1 = o_ytok + 1024
    o_end = o_t1 + 2048
    assert o_end <= NW * 4, (o_end, NW * 4)

    hT = A.view(o_hT, 16 * T, BF16).rearrange("p (c t) -> p c t", t=T)
    yT = A.view(o_yT, 8 * T, BF16).rearrange("p (c t) -> p c t", t=T)
    qT = A.view(o_qT, 2 * T, BF16).rearrange("p (c t) -> p c t", t=T)
    kT = A.view(o_kT, 2 * T, BF16).rearrange("p (c t) -> p c t", t=T)
    ktok = A.view(o_ktok, 16 * 256, BF16).rearrange("p (c e) -> p c e", e=256)
    vaug = A.view(o_vaug, 16 * 258, BF16).rearrange("p (c e) -> p c e", e=258)
    og = A.view(o_og, 16 * 256, BF16).rearrange("p (c e) -> p c e", e=256)
    hsum = A.view(o_hsum, 16 * 256, F32).rearrange("p (c e) -> p c e", e=256)
    ropeA = A.view(o_og, 1024, F32).rearrange("p (c e) -> p c e", e=64)
    ropeB = A.view(o_og + 4096, 1024, F32).rearrange("p (c e) -> p c e", e=64)
    Cst = [[A.view(o_Cst + (d * 2 + pp) * 2056, 514, F32).rearrange("p (c e) -> p c e", e=257)
            for pp in range(2)] for d in range(2)]
    Cbf = [A.view(o_Cbf + d * 1032, 516, BF16).rearrange("p (c e) -> p c e", e=258) for d in range(2)]
    STb = [A.view(o_STb + d * 256, 128, BF16) for d in range(2)]
    wkb = [A.view(o_wk + d * 512, 256, BF16) for d in range(2)]
    ytok = [A.view(o_ytok + i * 512, 256, BF16) for i in range(2)]
    t1 = [A.view(o_t1 + i * 1024, 256, F32) for i in range(2)]
    xtile0 = A.view(o_head, 16 * 512, F32).rearrange("p (c t) -> p c t", t=512)
    xtile1 = A.view(o_yT, 16 * 512, F32).rearrange("p (c t) -> p c t", t=512)
    xtiles = [xtile0, xtile1]
    sqt = [A.view(o_head + 32768 + i * 2048, 512, BF16) for i in range(2)]
    ones_b = A.view(o_head + 32768 + 10240, 128, BF16)
    rbt = A.view(o_head + 32768 + 4096, 512, F32)
    tmpn = [A.view(o_head + 32768 + 6144 + i * 2048, 512, F32) for i in range(2)]
    GI = A.view(o_head, T, F32, parts=4)
    GE = A.view(o_head + 8192, T, F32, parts=4)
    GB = A.view(o_head + 16384, T, F32, parts=4)
    keepT = A.view(o_head + 24576, T, F32, parts=4)
    NXO = 16
    xp_o = [A.view(o_head + i * 2048, 512, F32) for i in range(NXO)]
    h2T = hT
    actT = A.view(o_yT, 22 * T, BF16).rearrange("p (c t) -> p c t", t=T)
    o_ffx = o_yT + 22 * T * 2
    NXF = 5
    xp_f = [A.view(o_ffx + i * 2048, 512, F32) for i in range(NXF)]
    sat = [A.view(o_ffx + NXF * 2048 + i * 2048, 512, F32) for i in range(2)]
    assert o_ffx + NXF * 2048 + 4096 <= NW * 4

    PS = [nc.alloc_psum_tensor("ps%d" % i, [128, 512], F32) for i in range(8)]

    def psk(b):
        return ("ps", b)

    def barrier():
        P.add("dve", lambda e: e.memset(small[:, 15:16], 0.0), w=[PH, "small15"], phase=False)

    def dma(q, out, in_, r, w, key, phase=True, **kw):
        return P.add(q, lambda e: e.dma_start(out=out, in_=in_, **kw), r=r, w=w, dma=key, phase=phase)

    def mm(out, lhsT, rhs, start, stop, r, w):
        P.add("pe", lambda e: e.matmul(out, lhsT, rhs, start=start, stop=stop), r=r, w=w)

    def act(out, in_, func, r, w, bias=None, scale=None, accum_out=None):
        kw = {}
        if bias is not None:
            kw["bias"] = bias
        if scale is not None:
            kw["scale"] = scale
        if accum_out is not None:
            kw["accum_out"] = accum_out
        P.add("act", lambda e: e.activation(out=out, in_=in_, func=func, **kw), r=r, w=w)

    def tt(out, in0, in1, op, r, w, eng="dve"):
        P.add(eng, lambda e: e.tensor_tensor(out=out, in0=in0, in1=in1, op=op), r=r, w=w)

    def ts(out, in0, s1, s2, op0, op1, r, w, eng="dve"):
        if s2 is None:
            P.add(eng, lambda e: e.tensor_scalar(out=out, in0=in0, scalar1=s1, scalar2=None, op0=op0), r=r, w=w)
        else:
            P.add(eng, lambda e: e.tensor_scalar(out=out, in0=in0, scalar1=s1, scalar2=s2, op0=op0, op1=op1), r=r, w=w)

    def stt(out, in0, scalar, in1, op0, op1, r, w):
        P.add("dve", lambda e: e.scalar_tensor_tensor(out=out, in0=in0, scalar=scalar, in1=in1, op0=op0, op1=op1), r=r, w=w)

    def cp(eng, out, in_, r, w):
        if eng == "act":
            P.add("act", lambda e: e.copy(out=out, in_=in_), r=r, w=w)
        else:
            P.add(eng, lambda e: e.tensor_copy(out=out, in_=in_), r=r, w=w)

    class WS:
        def __init__(self):
            self.descs = []
            self.n = 0
            self.issued = 0

        def _issue(self, i):
            parts = self.descs[i]
            s = i % 2
            for (dst_fn, src) in parts:
                dst = dst_fn(wslot[s])
                dma("pool", dst, src, r=[], w=[("w", s)], key=("w", s), phase=False)
            self.issued = i + 1

        def get(self, parts):
            i = self.n
            self.n += 1
            if P.dry:
                self.descs.append(parts)
                return wslot[i % 2], ("w", i % 2)
            while self.issued <= min(i + 1, len(self.descs) - 1):
                self._issue(self.issued)
            return wslot[i % 2], ("w", i % 2)

    ws = WS()

    def wtile_kc(src3, kcn, ncols):
        return [(lambda sl: sl[:, 0:kcn * ncols].rearrange("p (k n) -> p k n", n=ncols), src3)]

    def wview(slot, kcn, ncols):
        return slot[:, 0:kcn * ncols].rearrange("p (k n) -> p k n", n=ncols)

    def xsrc(l, first):
        return x_in if first else xT

    def program():
        ws.n = 0
        cload = []
        cload.append(dma("sp", ident_f[:, :], cst_f[:, 0:128], [], ["ident_f"], "c0"))
        dma("sp", ones_f[:, :], cst_f[:, 128:256], [], ["ones_f"], "c0")
        dma("sp", sel[0:4, :], cst_f[0:4, 256:320], [], ["sel"], "c0")
        dma("sp", jtab[:, :], cst_f[:, 320:324], [], ["jtab"], "c0")
        dma("sp", ident_b[:, :], cst_b[:, 0:128], [], ["ident_b"], "c0")
        dma("sp", mask_b2[0][:, :], cst_b[:, 128:256], [], ["mask0"], "c0")
        dma("sp", mask_b2[1][:, :], cst_b[:, 256:384], [], ["mask1"], "c0")
        dma("sp", vecs[:, :], vec_fm[:, :], [], ["vecs"], "c0")
        dma("sp", kapbc[:, :], kap_in.partition_broadcast(128), [], ["kapbc"], "c0")
        dma("sp", gb4[0:4, :], gb_in[:, :], [], ["gb4"], "c0")
        dma("sp", m04[0:4, :], m0_in[:, :], [], ["m04"], "c0")
        dma("sp", rope_t[:, :], rope_in[:, :], [], ["rope"], "c0")
        P.add("dve", lambda e: e.memset(epsT[:, :], EPS), w=["eps"])
        ts(gb4[0:4, 16:32], gb4[0:4, 0:16], -1.0, None, ALU.mult, None, r=["gb4"], w=["ngb4"])
        act(csil[:, :], vecs[:, 144:160], AF.Silu, r=["vecs"], w=["csil"])
        cp("dve", sT[:, :], csil[:, :], r=["csil"], w=["sT"])

        P.scope = "mod"
        psm = PS[7]

        def mod_gen(l):
            wv = w_mod[l].rearrange("(k p) n -> p k n", p=128)
            for jt in range(48):
                sc_save = P.scope
                P.scope = "mod"
                slot, wk_ = ws.get(wtile_kc(wv[:, :, jt * 256:(jt + 1) * 256], 16, 256))
                wt = wview(slot, 16, 256)
                for cg in range(2):
                    j = jt * 2 + cg
                    for kc in range(16):
                        mm(psm[:, l * 96 + j:l * 96 + j + 1], wt[:, kc, cg * 128:(cg + 1) * 128], sT[:, kc:kc + 1],
                           kc == 0, kc == 15, r=[wk_, "sT"], w=[psk(7)])
                if jt == 47:
                    tt(modv[:, l * 96:(l + 1) * 96], psm[:, l * 96:(l + 1) * 96],
                       vecs[:, 160 + l * 96:160 + (l + 1) * 96], ALU.add, r=[psk(7), "vecs"], w=[("modv", l)])
                    mA_ = modA[:, l * 96:(l + 1) * 96]
                    mv_ = modv[:, l * 96:(l + 1) * 96]
                    stt(mA_[:, 0:16], mv_[:, 16:32], 1.0, vecs[:, l * 16:(l + 1) * 16], ALU.add, ALU.mult,
                        r=[("modv", l), "vecs"], w=[("modA", l)])
                    stt(mA_[:, 48:64], mv_[:, 64:80], 1.0, vecs[:, 64 + l * 16:64 + (l + 1) * 16], ALU.add, ALU.mult,
                        r=[("modv", l), "vecs"], w=[("modA", l)])
                P.scope = sc_save
                yield

        for _ in mod_gen(0):
            pass
        barrier()

        def norm_phase(src, Acol, Bcol, dst, rkeys_fn, l):
            P.add("dve", lambda e: e.memset(ones_b[:, :], 1.0), w=["ones_b"])
            for tt_ in range(4):
                tsl = slice(tt_ * 512, (tt_ + 1) * 512)
                xtile = xtiles[tt_ % 2]
                xk = "xtile%d" % (tt_ % 2)
                dma("sp", xtile[:, :, :], src.rearrange("c p t -> p c t")[:, :, tsl],
                    r=[("xT", dc, tt_) for dc in range(16)], w=[xk], key="xt%d" % (tt_ % 2))
                psq = PS[tt_ % 2]
                for dc in range(16):
                    sq = sqt[dc % 2]
                    act(sq[:, :], xtile[:, dc, :], AF.Square, r=[xk], w=[("sq", dc % 2)])
                    mm(psq[:, :], ones_b[:, :], sq[:, :], dc == 0, dc == 15, r=[("sq", dc % 2), "ones_b"],
                       w=[psk(tt_ % 2)])
                act(rbt[:, :], psq[:, :], AF.Sqrt, r=[psk(tt_ % 2), "eps"], w=["rbt"], bias=epsT[:, 0:1], scale=1.0 / D)
                P.add("dve", lambda e: e.reciprocal(out=rbt[:, :], in_=rbt[:, :]), r=["rbt"], w=["rbt"])
                for dc in range(16):
                    tm = tmpn[dc % 2]
                    tt(tm[:, :], xtile[:, dc, :], rbt[:, :], ALU.mult, r=[xk, "rbt"], w=[("tmpn", dc % 2)])
                    act(dst[:, dc, tsl], tm[:, :], AF.Identity, r=[("tmpn", dc % 2), ("modA", l)], w=[("hT", tt_)],
                        scale=Acol[:, dc:dc + 1], bias=Bcol[:, dc:dc + 1])

        def rmw_x(psb, dc, tt_, gcol, src, xp, xpk, bank):
            tsl = slice(tt_ * 512, (tt_ + 1) * 512)
            dma("sp", xp[:, :], src[dc][:, tsl], r=[("xT", dc, tt_)], w=[xpk], key=("L",) + xpk)
            stt(xp[:, :], psb[:, :], gcol, xp[:, :], ALU.mult, ALU.add, r=[psk(bank), xpk], w=[xpk])
            dma("act", xT[dc][:, tsl], xp[:, :], r=[xpk], w=[("xT", dc, tt_)], key=("S",) + xpk)

        for l in range(nl):
            first = (l == 0)
            mA = modA[:, l * 96:(l + 1) * 96]
            mv = modv[:, l * 96:(l + 1) * 96]
            win = w_in[l].rearrange("(k p) n -> p k n", p=128)
            mgen = mod_gen(l + 1) if l + 1 < nl else None

            def pump(mgen=mgen):
                if mgen is not None:
                    next(mgen, None)

            P.scope = "norm1"
            norm_phase(x_in if first else xT, mA[:, 0:16], mv[:, 0:16], hT, None, l)
            barrier()
            P.scope = "rtab"
            r8 = slice(l * 8, (l + 1) * 8)
            dma("sp", lgb[:, :], rdec_in[:, r8].partition_broadcast(128), [], ["lgb"], "k_lgb")
            act(lgb[:, :], lgb[:, :], AF.Exp, r=["lgb"], w=["lgb"])
            ts(lgb[:, :], lgb[:, :], -1.0, None, ALU.mult, None, r=["lgb"], w=["lgb"])
            for d in range(2):
                d4 = slice(d * 4, d * 4 + 4)
                ts(rs_row[:, d4], lgb[:, d4], jtab[:, d:d + 1], None, ALU.mult, None, r=["lgb", "jtab"], w=["rs_row"])
                ts(rs_wst[:, d4], lgb[:, d4], jtab[:, 2 + d:3 + d], None, ALU.mult, None, r=["lgb", "jtab"], w=["rs_wst"])
            act(rs_wsc[:, :], rs_row[:, :], AF.Exp, r=["rs_row"], w=["rs_wsc"], scale=-1.0)
            act(rs_row[:, :], rs_row[:, :], AF.Exp, r=["rs_row", "rs_wsc"], w=["rs_row"])
            act(rs_wst[:, :], rs_wst[:, :], AF.Exp, r=["rs_wst"], w=["rs_wst"])
            act(rs_a128[:, :], lgb[:, :], AF.Exp, r=["lgb"], w=["rs_a128"], scale=128.0)
            for d in range(2):
                for h in range(4):
                    i8 = d * 4 + h
                    ts(rs_a128k[:, i8 * 16:(i8 + 1) * 16], kapbc[:, d * 16:(d + 1) * 16], rs_a128[:, i8:i8 + 1], None,
                       ALU.mult, None, r=["kapbc", "rs_a128"], w=["rs_a128k"])

            for grp in range(2):
                if grp == 0:
                    P.scope = "gates"
                    dma("sp", keepT[:, :], keep_in[:, :], [], ["keepT"], "k_keep")
                    gsrc = win[:, :, 8192:8208]
                    slot, wk_ = ws.get([(lambda sl: sl[:, 0:256].rearrange("p (k n) -> p k n", n=16), gsrc)])
                    wg = slot[:, 0:256].rearrange("p (k n) -> p k n", n=16)
                    for d in range(2):
                        for ty in range(2):
                            tyi = d * 2 + ty
                            for tt_ in range(4):
                                tsl = slice(tt_ * 512, (tt_ + 1) * 512)
                                pb = (tyi * 4 + tt_) % 2
                                for kc in range(16):
                                    mm(PS[pb][0:4, :], wg[:, kc, tyi * 4:tyi * 4 + 4], hT[:, kc, tsl], kc == 0, kc == 15,
                                       r=[wk_, ("hT", tt_)], w=[psk(pb)])
                                bcol = l * 4 + tyi
                                if ty == 0:
                                    ts(GI[:, tsl], PS[pb][0:4, :], gb4[0:4, bcol:bcol + 1], None, ALU.add, None,
                                       r=[psk(pb), "gb4"], w=["GI"])
                                else:
                                    act(GE[:, tsl], PS[pb][0:4, :], AF.Exp, r=[psk(pb), "ngb4"], w=["GE"],
                                        scale=-1.0, bias=gb4[0:4, 16 + bcol:17 + bcol])
                        act(GE[:, :], GE[:, :], AF.Ln, r=["GE"], w=["GE"], bias=1.0)
                        P.add("dve", lambda e: e.tensor_tensor_scan(out=GB[:, :], data0=keepT[:, :], data1=GE[:, :],
                                                                     initial=0.0, op0=ALU.mult, op1=ALU.add),
                              r=["keepT", "GE"], w=["GB"])
                        GI3 = GI.rearrange("p (c t) -> p c t", t=128)
                        GE3 = GE.rearrange("p (c t) -> p c t", t=128)
                        GB3 = GB.rearrange("p (c t) -> p c t", t=128)
                        cp("dve", g_Bl[0:4, :].rearrange("p (c o) -> p c o", o=1), GB3[:, :, 127:128], r=["GB"], w=["gBl"])
                        if d == 0:
                            BP = GB
                            BP3 = GB3
                            bpk = "GB"
                        else:
                            tt(GE[:, :], GE[:, :], GB[:, :], ALU.subtract, r=["GE", "GB"], w=["GE"])
                            tt(GE3, GE3, GB3[:, :, 127:128].to_broadcast([4, 16, 128]), ALU.add, r=["GE", "GB"], w=["GE"])
                            BP = GE
                            BP3 = GE3
                            bpk = "GE"
                        tt(GI[:, :], GI[:, :], BP[:, :], ALU.add, r=["GI", bpk], w=["GI"])
                        P.add("dve", lambda e: e.tensor_reduce(out=g_umax[0:4, :], in_=GI3, axis=AX.X, op=ALU.max),
                              r=["GI"], w=["gumax"])
                        order = list(range(16)) if d == 0 else list(range(15, -1, -1))
                        prev = None
                        for c in order:
                            if prev is None:
                                cp("dve", g_mk[0:4, c:c + 1], m04[0:4, l * 2 + d:l * 2 + d + 1], r=["m04"], w=["gmk"])
                            else:
                                tt(g_mk[0:4, c:c + 1], g_m[0:4, prev:prev + 1], kapbc[0:4, d * 16 + c:d * 16 + c + 1],
                                   ALU.mult, r=["gm", "kapbc"], w=["gmk"])
                            tt(g_M[0:4, c:c + 1], g_mk[0:4, c:c + 1], g_umax[0:4, c:c + 1], ALU.max,
                               r=["gmk", "gumax"], w=["gM"])
                            tt(g_m[0:4, c:c + 1], g_M[0:4, c:c + 1], g_Bl[0:4, c:c + 1], ALU.subtract,
                               r=["gM", "gBl"], w=["gm"])
                            prev = c
                        tt(g_ast[0:4, :], g_mk[0:4, :], g_M[0:4, :], ALU.subtract, r=["gmk", "gM"], w=["gast"])
                        act(g_ast[0:4, :], g_ast[0:4, :], AF.Exp, r=["gast"], w=["gast"])
                        tt(g_ast[0:4, :], g_ast[0:4, :], kapbc[0:4, d * 16:(d + 1) * 16], ALU.mult,
                           r=["gast", "kapbc"], w=["gast"])
                        Mb = g_M[0:4, :].rearrange("p (c o) -> p c o", o=1).to_broadcast([4, 16, 128])
                        tt(GI3, GI3, Mb, ALU.subtract, r=["GI", "gM"], w=["GI"])
                        act(GI[:, :], GI[:, :], AF.Exp, r=["GI"], w=["GI"])
                        tt(BP3, BP3, Mb, ALU.subtract, r=[bpk, "gM"], w=[bpk])
                        act(BP[:, :], BP[:, :], AF.Exp, r=[bpk], w=[bpk])
                        for q, (src_, sk) in enumerate(((GI, "GI"), (BP, bpk))):
                            for c in range(16):
                                o0 = q * 64 + c * 4
                                mm(PS[2][:, o0:o0 + 4], src_[0:4, c * 128:(c + 1) * 128], ident_f[0:4, 0:4], True, True,
                                   r=[sk, "ident_f"], w=[psk(2)])
                        cp("dve", gtok[:, d * 128:(d + 1) * 128], PS[2][:, 0:128], r=[psk(2)], w=[("gtok", d)])
                        tt(g_a4[0:4, :].rearrange("p (h c) -> p h c", c=16), sel[0:4, :].rearrange("p (h c) -> p h c", c=16),
                           g_ast[0:4, :].rearrange("p (o c) -> p o c", o=1).to_broadcast([4, 4, 16]), ALU.mult,
                           r=["sel", "gast"], w=["ga4"])
                        mm(PS[3][:, 0:64], ones_f[0:4, :], g_a4[0:4, :], True, True, r=["ga4", "ones_f"], w=[psk(3)])
                        cp("dve", gbc[:, d * 64:(d + 1) * 64], PS[3][:, 0:64], r=[psk(3)], w=[("gbc", d)])
                        g_m3 = g_m[0:4, :].rearrange("p (s two) -> p s two", two=2)
                        dma("sp", oM[:, l, d, :], g_m3[:, :, 1 - d], r=["gm"], w=[("oM", l, d)], key="om",
                            allow_slow_non_contiguous=True)
                    barrier()

                for hh in range(4):
                    base = grp * 4096
                    cols = [base + i * 1024 + hh * 256 for i in range(4)]
                    gsrc = (mng if grp == 0 else rng_)[l:l + 1, hh * 256:(hh + 1) * 256]
                    dma("sp", Gn[:, :], gsrc.partition_broadcast(128), [], ["Gn"], "k_gn")
                    P.add("dve", lambda e: e.memset(vaug[:, :, 256:258], 1.0), w=["vaug1"])
                    P.scope = "proj%d" % grp
                    pend = []
                    gidx = [0]
                    tcnt = [0]

                    def flush(upto):
                        while pend and (upto is None or pend[0][0] <= upto):
                            pend.pop(0)[1]()

                    def emit_tr(j, wi):
                        dstT = qT if wi == 0 else kT
                        dkey = "qT" if wi == 0 else "kT"
                        jsl = slice(j * 128, (j + 1) * 128)
                        pb = 4 + tcnt[0] % 3
                        tcnt[0] += 1
                        for dc in range(2):
                            P.add("pe", lambda e, pb=pb, dc=dc, j=j: e.transpose(
                                PS[pb][:, dc * 128:(dc + 1) * 128], hsum[:, j, dc * 128:(dc + 1) * 128], ident_f[:, :]),
                                r=[("hsum", j), "ident_f"], w=[psk(pb)])
                        cp("dve", dstT[:, :, jsl], PS[pb][:, 0:256].rearrange("p (c t) -> p c t", t=128),
                           r=[psk(pb)], w=[dkey])

                    LAG = 2 if grp == 0 else 4
                    for wi in range(4):
                        slot, wk_ = ws.get(wtile_kc(win[:, :, cols[wi]:cols[wi] + 256], 16, 256))
                        wt = wview(slot, 16, 256)
                        for c in range(16):
                            csl = slice(c * 128, (c + 1) * 128)
                            pb = c % 4
                            for kc in range(16):
                                mm(PS[pb][:, 0:256], hT[:, kc, csl], wt[:, kc, :], kc == 0, kc == 15,
                                   r=[wk_, ("hT", c // 4)], w=[psk(pb)])
                            if wi < 2:
                                sc = 1.0
                                if (grp == 0 and wi == 1) or (grp == 1 and wi == 0):
                                    sc = 1.0 / 16.0
                                act(hsum[:, c, :], PS[pb][:, 0:256], AF.Copy, r=[psk(pb)], w=[("hsum", c)], scale=sc)
                                ready = []
                                if grp == 0:
                                    ready = [c]
                                elif c % 4 == 3:
                                    g4 = slice(c - 3, c + 1)
                                    gk = [("hsum", j) for j in range(c - 3, c + 1)]
                                    for half in range(2):
                                        x1 = hsum[:, g4, half * 128:half * 128 + 64]
                                        x2 = hsum[:, g4, half * 128 + 64:half * 128 + 128]
                                        if half == 0:
                                            cs = rope_t[:, 0:1024].rearrange("p (c e) -> p c e", e=64)[:, g4, :]
                                            sn = rope_t[:, 1024:2048].rearrange("p (c e) -> p c e", e=64)[:, g4, :]
                                        else:
                                            cs = rope_t[:, 2048:2112].rearrange("p (o e) -> p o e", o=1).to_broadcast([128, 4, 64])
                                            sn = rope_t[:, 2112:2176].rearrange("p (o e) -> p o e", o=1).to_broadcast([128, 4, 64])
                                        rA = ropeA[:, 0:4, :]
                                        rB = ropeB[:, 0:4, :]
                                        tt(rA, x2, sn, ALU.mult, r=gk + ["rope"], w=["og"])
                                        tt(rB, x1, sn, ALU.mult, r=gk + ["rope", "og"], w=["og"])
                                        tt(x1, x1, cs, ALU.mult, r=gk + ["rope", "og"], w=gk)
                                        tt(x1, x1, rA, ALU.subtract, r=gk + ["og"], w=gk)
                                        tt(x2, x2, cs, ALU.mult, r=gk + ["rope"], w=gk)
                                        tt(x2, x2, rB, ALU.add, r=gk + ["og"], w=gk)
                                    ready = list(range(c - 3, c + 1))
                                if ready:
                                    if wi == 1:
                                        rs_ = slice(ready[0], ready[-1] + 1)
                                        cp("act", ktok[:, rs_, :], hsum[:, rs_, :], r=[("hsum", j) for j in ready], w=["ktok"])
                                    for j in ready:
                                        pend.append((gidx[0] + LAG, lambda j=j, wi=wi: emit_tr(j, wi)))
                            elif wi == 2:
                                cp("dve", vaug[:, c, 0:256], PS[pb][:, 0:256], r=[psk(pb)], w=[("vaug", c)])
                            else:
                                act(og[:, c, :], PS[pb][:, 0:256], AF.Sigmoid if grp == 0 else AF.Silu,
                                    r=[psk(pb)], w=["og"])
                            gidx[0] += 1
                            flush(gidx[0])
                    flush(None)
                    P.scope = "scan%d" % grp
                    cur = [0, 0]
                    for d in range(2):
                        src = (initC if grp == 0 else initS)[l, d, hh].rearrange("(c p) e -> p c e", p=128)
                        ne = 257 if grp == 0 else 256
                        dma("sp", Cst[d][0][:, :, 0:ne], src, [], [("Cst", d, 0)], ("ci", d))
                    written = [False] * 16
                    for step in range(16):
                        for d in range(2):
                            c = step if d == 0 else 15 - step
                            csl = slice(c * 128, (c + 1) * 128)
                            pp = cur[d]
                            ne = 257 if grp == 0 else 256
                            if grp == 0:
                                gt = gtok[:, d * 128:(d + 1) * 128]
                                wsc = gt[:, c * 4 + hh:c * 4 + hh + 1]
                                wst = wsc
                                flo = gt[:, 64 + c * 4 + hh:64 + c * 4 + hh + 1]
                                a_in = gbc[:, d * 64 + hh * 16 + c:d * 64 + hh * 16 + c + 1]
                                a_st = a_in
                                skeys = [("gtok", d), ("gbc", d)]
                            else:
                                i8 = d * 4 + hh
                                wsc = rs_wsc[:, i8:i8 + 1]
                                wst = rs_wst[:, i8:i8 + 1]
                                row = rs_row[:, i8:i8 + 1]
                                a_in = kapbc[:, d * 16 + c:d * 16 + c + 1]
                                a_st = rs_a128k[:, i8 * 16 + c:i8 * 16 + c + 1]
                                skeys = ["rs_wsc", "rs_wst", "rs_row", "kapbc", "rs_a128k"]
                            act(Cbf[d][:, :, 0:ne], Cst[d][pp][:, :, 0:ne], AF.Copy, r=[("Cst", d, pp)] + skeys,
                                w=[("Cbf", d)], scale=a_in)
                            for dc in range(2):
                                mm(PS[d][:, 0:128], kT[:, dc, csl], qT[:, dc, csl], dc == 0, dc == 1,
                                   r=["kT", "qT"], w=[psk(d)])
                            stt(STb[d][:, :], PS[d][:, 0:128], wsc, mask_b2[d][:, :], ALU.mult, ALU.mult,
                                r=[psk(d), "mask%d" % d] + skeys, w=[("STb", d)])
                            for dc in range(2):
                                mm(PS[2 + d][:, 0:ne], qT[:, dc, csl], Cbf[d][:, dc, 0:ne], dc == 0, False,
                                   r=["qT", ("Cbf", d)], w=[psk(2 + d)])
                            mm(PS[2 + d][:, 0:ne], STb[d][:, :], vaug[:, c, 0:ne], False, True,
                               r=[("STb", d), ("vaug", c), "vaug1"], w=[psk(2 + d)])
                            if grp == 0:
                                den = small[:, d * 2:d * 2 + 1]
                                act(den, PS[2 + d][:, 256:257], AF.Abs, r=[psk(2 + d)], w=[("den", d)])
                                tt(den, den, flo, ALU.max, r=[("den", d)] + skeys, w=[("den", d)])
                                P.add("dve", lambda e, den=den: e.reciprocal(out=den, in_=den), r=[("den", d)], w=[("den", d)])
                                rsc = den
                                rk = [("den", d)]
                            else:
                                rsc = row
                                rk = skeys
                            if not written[c]:
                                act(hsum[:, c, :], PS[2 + d][:, 0:256], AF.Copy, r=[psk(2 + d)] + rk, w=[("hsum", c)],
                                    scale=rsc)
                                written[c] = True
                                fin = False
                            else:
                                stt(hsum[:, c, :], PS[2 + d][:, 0:256], rsc, hsum[:, c, :], ALU.mult, ALU.add,
                                    r=[psk(2 + d), ("hsum", c)] + rk, w=[("hsum", c)])
                                fin = True
                            ts(wkb[d][:, :], ktok[:, c, :], wst, None, ALU.mult, None, r=["ktok"] + skeys, w=[("wk", d)],
                               eng="pool")
                            for dc in range(2):
                                mm(PS[4 + dc][:, 0:ne], wkb[d][:, dc * 128:(dc + 1) * 128], vaug[:, c, 0:ne], True, True,
                                   r=[("wk", d), ("vaug", c), "vaug1"], w=[psk(4 + dc)])
                            for dc in range(2):
                                stt(Cst[d][1 - pp][:, dc, 0:ne], Cst[d][pp][:, dc, 0:ne], a_st, PS[4 + dc][:, 0:ne],
                                    ALU.mult, ALU.add, r=[("Cst", d, pp), psk(4 + dc)] + skeys, w=[("Cst", d, 1 - pp)])
                            cur[d] = 1 - pp
                            if (d == 0 and c % 2 == 1) or (d == 1 and c % 2 == 0):
                                sq_ = c // 2
                                if grp == 0:
                                    dst = oC[sq_, l, d, hh].rearrange("(c p) e -> p c e", p=128)
                                else:
                                    dst = oS[sq_, l, d, hh].rearrange("(c p) e -> p c e", p=128)
                                dma("sp", dst, Cst[d][1 - pp][:, :, 0:ne], r=[("Cst", d, 1 - pp)],
                                    w=[("ost", grp, l, d, hh, sq_)], key=("so", d, 1 - pp))
                            if fin:
                                i2 = c % 2
                                ssq = small[:, 4 + i2:5 + i2]
                                act(t1[i2][:, :], hsum[:, c, :], AF.Square, r=[("hsum", c)], w=[("t1", i2), ("ssq", i2)],
                                    accum_out=ssq)
                                act(ssq, ssq, AF.Sqrt, r=[("ssq", i2), "eps"], w=[("ssq", i2)], bias=epsT[:, 0:1],
                                    scale=1.0 / 256.0)
                                P.add("dve", lambda e, ssq=ssq: e.reciprocal(out=ssq, in_=ssq), r=[("ssq", i2)], w=[("ssq", i2)])
                                stt(t1[i2][:, :], hsum[:, c, :], ssq, og[:, c, :], ALU.mult, ALU.mult,
                                    r=[("hsum", c), ("ssq", i2), "og"], w=[("t1", i2)])
                                tt(ytok[i2][:, :], t1[i2][:, :], Gn[:, :], ALU.mult, r=[("t1", i2), "Gn"], w=[("ytok", i2)])
                                psy = PS[6][:, i2 * 128:(i2 + 1) * 128].bitcast(BF16).rearrange("p (c t) -> p c t", t=128)
                                for dc in range(2):
                                    P.add("pe", lambda e, psy=psy, dc=dc, i2=i2: e.transpose(
                                        psy[:, dc, :], ytok[i2][:, dc * 128:(dc + 1) * 128], ident_b[:, :]),
                                        r=[("ytok", i2), "ident_b"], w=[psk(6)])
                                cp("act", yT[:, hh * 2:hh * 2 + 2, csl], psy, r=[psk(6)], w=["yT"])
                        if step % 3 == 2 or step == 0:
                            pump()
                barrier()
                P.scope = "outproj"
                wo = w_out[l, grp * 1024:(grp + 1) * 1024, :].rearrange("(k p) n -> p k n", p=128)
                n = 0
                for jt in range(8):
                    slot, wk_ = ws.get(wtile_kc(wo[:, :, jt * 256:(jt + 1) * 256], 8, 256))
                    wt = wview(slot, 8, 256)
                    for cg in range(2):
                        dc_ = jt * 2 + cg
                        for tt_ in range(4):
                            tsl = slice(tt_ * 512, (tt_ + 1) * 512)
                            pb = n % 7
                            for kc in range(8):
                                mm(PS[pb][:, :], wt[:, kc, cg * 128:(cg + 1) * 128], yT[:, kc, tsl], kc == 0, kc == 7,
                                   r=[wk_, "yT"], w=[psk(pb)])
                            rmw_x(PS[pb], dc_, tt_, mv[:, 32 + dc_:33 + dc_], x_in if (first and grp == 0) else xT,
                                  xp_o[n % NXO], ("xpo", n % NXO), pb)
                            n += 1
                barrier()

            if mgen is not None:
                for _ in mgen:
                    pass
            P.scope = "norm2"
            norm_phase(xT, mA[:, 48:64], mv[:, 48:64], h2T, None, l)
            barrier()
            wgu = w_gu[l].rearrange("(k p) n -> p k n", p=128)
            for half in range(2):
                n = 0
                P.scope = "ffn_gu"
                for hc in range(22):
                    hg = half * 22 + hc
                    parts = [
                        (lambda sl: sl[:, 0:4096].rearrange("p (k n) -> p k n", n=256)[:, :, 0:128],
                         wgu[:, :, hg * 128:(hg + 1) * 128]),
                        (lambda sl: sl[:, 0:4096].rearrange("p (k n) -> p k n", n=256)[:, :, 128:256],
                         wgu[:, :, DFF + hg * 128:DFF + (hg + 1) * 128]),
                    ]
                    slot, wk_ = ws.get(parts)
                    wt = wview(slot, 16, 256)
                    for tt_ in range(4):
                        tsl = slice(tt_ * 512, (tt_ + 1) * 512)
                        pa = (2 * n) % 6
                        pu = (2 * n + 1) % 6
                        for kc in range(16):
                            mm(PS[pa][:, :], wt[:, kc, 0:128], h2T[:, kc, tsl], kc == 0, kc == 15,
                               r=[wk_, ("hT", tt_)], w=[psk(pa)])
                        for kc in range(16):
                            mm(PS[pu][:, :], wt[:, kc, 128:256], h2T[:, kc, tsl], kc == 0, kc == 15,
                               r=[wk_, ("hT", tt_)], w=[psk(pu)])
                        sa = sat[n % 2]
                        act(sa[:, :], PS[pa][:, :], AF.Silu, r=[psk(pa)], w=[("sat", n % 2)])
                        tt(actT[:, hc, tsl], sa[:, :], PS[pu][:, :], ALU.mult, r=[("sat", n % 2), psk(pu)], w=["actT"])
                        n += 1
                P.scope = "ffn_down"
                wdn = w_down[l, half * 22 * 128:(half + 1) * 22 * 128, :].rearrange("(k p) n -> p k n", p=128)
                n = 0
                for dc_ in range(16):
                    slot, wk_ = ws.get(wtile_kc(wdn[:, :, dc_ * 128:(dc_ + 1) * 128], 22, 128))
                    wt = wview(slot, 22, 128)
                    for tt_ in range(4):
                        tsl = slice(tt_ * 512, (tt_ + 1) * 512)
                        pb = n % 6
                        for kc in range(22):
                            mm(PS[pb][:, :], wt[:, kc, :], actT[:, kc, tsl], kc == 0, kc == 21,
                               r=[wk_, "actT"], w=[psk(pb)])
                        rmw_x(PS[pb], dc_, tt_, mv[:, 80 + dc_:81 + dc_], xT, xp_f[n % NXF], ("xpf", n % NXF), pb)
                        n += 1
            barrier()

        P.scope = "final"
        fg = vecs[:, 128:144]
        srcx = xT if nl > 0 else x_in
        P.add("dve", lambda e: e.memset(ones_b[:, :], 1.0), w=["ones_b"])
        for tt_ in range(4):
            tsl = slice(tt_ * 512, (tt_ + 1) * 512)
            xtile = xtiles[tt_ % 2]
            xk = "xtile%d" % (tt_ % 2)
            dma("sp", xtile[:, :, :], srcx.rearrange("c p t -> p c t")[:, :, tsl], r=[("xT", dc, tt_) for dc in range(16)], w=[xk], key="xt%d" % (tt_ % 2))
            psq = PS[tt_ % 2]
            for dc in range(16):
                sq = sqt[dc % 2]
                act(sq[:, :], xtile[:, dc, :], AF.Square, r=[xk], w=[("sq", dc % 2)])
                mm(psq[:, :], ones_b[:, :], sq[:, :], dc == 0, dc == 15, r=[("sq", dc % 2), "ones_b"], w=[psk(tt_ % 2)])
            act(rbt[:, :], psq[:, :], AF.Sqrt, r=[psk(tt_ % 2), "eps"], w=["rbt"], bias=epsT[:, 0:1], scale=1.0 / D)
            P.add("dve", lambda e: e.reciprocal(out=rbt[:, :], in_=rbt[:, :]), r=["rbt"], w=["rbt"])
            for dc in range(16):
                stt(xtile[:, dc, :], xtile[:, dc, :], fg[:, dc:dc + 1], rbt[:, :], ALU.mult, ALU.mult,
                    r=[xk, "rbt", "vecs"], w=[xk])
            dma("sp", yT_out.rearrange("c p t -> p c t")[:, :, tsl], xtile[:, :, :], r=[xk], w=[("yout", tt_)], key="xt%d" % (tt_ % 2))

    P.dry = True
    program()
    P.dry = False
    program()
    P.emit(nc)
    return nc


_NC_CACHE = {}


def _consts():
    cf = np.zeros((128, 1536), np.float32)
    cf[:, 0:128] = np.eye(128, dtype=np.float32)
    cf[:, 128:256] = 1.0
    selm = np.zeros((4, 4, 16), np.float32)
    for h in range(4):
        selm[h, h, :] = 1.0
    cf[0:4, 256:320] = selm.reshape(4, 64)
    t = np.arange(128, dtype=np.float32)
    cf[:, 320] = t + 1.0
    cf[:, 321] = 128.0 - t
    cf[:, 322] = 127.0 - t
    cf[:, 323] = t
    cb = np.zeros((128, 384), np.float32)
    cb[:, 0:128] = np.eye(128)
    s = np.arange(128)[:, None]
    tt = np.arange(128)[None, :]
    cb[:, 128:256] = (tt >= s)
    cb[:, 256:384] = (tt <= s)
    keep = np.ones((4, T), np.float32)
    keep[:, ::128] = 0.0
    return cf, cb.astype(ml_dtypes.bfloat16), keep


def _rope(latent):
    out = np.zeros((128, 2 * 1024 + 128), np.float32)
    if not latent:
        out[:, 0:1024] = 1.0
        out[:, 2048:2112] = 1.0
        return out
    inv = (10000.0 ** (-np.arange(64, dtype=np.float32) / 64.0)).astype(np.float32)
    p = np.arange(128)
    c = np.arange(16)
    r = (2 * c[None, :] + (p[:, None] // 64)).astype(np.float32)
    col = (p % 64).astype(np.float32)
    ang_r = (r[:, :, None] * inv[None, None, :]).astype(np.float32)
    ang_c = (col[:, None] * inv[None, :]).astype(np.float32)
    out[:, 0:1024] = np.cos(ang_r).reshape(128, 1024)
    out[:, 1024:2048] = np.sin(ang_r).reshape(128, 1024)
    out[:, 2048:2112] = np.cos(ang_c)
    out[:, 2112:2176] = np.sin(ang_c)
    return out


def kernel(x_prompt, x_sample, c, state_mlstm_C, state_mlstm_n, state_mlstm_m, state_ret_S,
           c_ctx, w_mod, b_mod, norm1_g, w_in, mlstm_gate_b, ret_decay, mlstm_norm_g, ret_norm_g,
           w_out, norm2_g, w_gu, w_down, final_norm_g, _nl=NL, _cores=None, _trace=False, _scopes=False):
    f = lambda a: np.ascontiguousarray(np.asarray(a, dtype=np.float32))
    x_prompt, x_sample, c, c_ctx = f(x_prompt), f(x_sample), f(c), f(c_ctx)
    nl = _nl
    if (nl, _scopes) not in _NC_CACHE:
        _NC_CACHE[(nl, _scopes)] = build(nl, _scopes)
    nc = _NC_CACHE[(nl, _scopes)]
    cf, cb, keep = _consts()

    def fm(v):
        v = np.asarray(v, np.float32)
        return v.reshape(v.shape[:-1] + (16, 128))

    shared = {
        "w_mod": f(w_mod), "w_in": f(w_in), "w_out": f(w_out), "w_gu": f(w_gu), "w_down": f(w_down),
        "mnorm_g": f(mlstm_norm_g), "rnorm_g": f(ret_norm_g), "cst_f": cf, "cst_b": cb, "keep": keep,
        "rdec": f(ret_decay).reshape(1, 32),
    }
    gbh = np.zeros((4, 32), np.float32)
    gbh[:, 0:16] = np.transpose(f(mlstm_gate_b), (2, 0, 1)).reshape(4, 16)
    shared["gb"] = gbh
    n1 = np.transpose(fm(norm1_g), (2, 0, 1)).reshape(128, 64)
    n2 = np.transpose(fm(norm2_g), (2, 0, 1)).reshape(128, 64)
    fgm = fm(final_norm_g).T
    bm = np.transpose(f(b_mod).reshape(4, 96, 128), (2, 0, 1)).reshape(128, 384)
    rope_lat, rope_ctx = _rope(True), _rope(False)
    in_maps = []
    for core in range(8):
        m = dict(shared)
        latent = core < 4
        if latent:
            b = core
            xs = x_sample[b]
            cv = c[b]
            Caug = np.concatenate([f(state_mlstm_C)[b], f(state_mlstm_n)[b][..., None]], axis=-1)
            S0 = f(state_ret_S)[b]
            m0 = f(state_mlstm_m)[b]
            kap = np.ones((1, 32), np.float32)
        else:
            j = core - 4
            xs = x_prompt[8 * j:8 * j + 8].reshape(T, D)
            cv = c_ctx
            Caug = np.zeros((NL, 2, 4, 256, 257), np.float32)
            S0 = np.zeros((NL, 2, 4, 256, 256), np.float32)
            m0 = np.zeros((NL, 2, 4), np.float32)
            kap = np.ones((1, 32), np.float32)
            kap[0, 0:16:2] = 0.0
            kap[0, 17:32:2] = 0.0
            kap[0, 0] = 1.0
            kap[0, 31] = 1.0
        m["x_in"] = np.ascontiguousarray(xs.T.reshape(16, 128, T))
        vec = np.zeros((128, 544), np.float32)
        vec[:, 0:64] = n1
        vec[:, 64:128] = n2
        vec[:, 128:144] = fgm
        vec[:, 144:160] = cv.reshape(16, 128).T
        vec[:, 160:544] = bm
        m["vec_fm"] = vec
        m["m0"] = np.ascontiguousarray(np.transpose(m0, (2, 0, 1)).reshape(4, 8))
        m["kap"] = kap
        m["init_Caug"] = np.ascontiguousarray(Caug)
        m["init_S"] = np.ascontiguousarray(S0)
        m["rope"] = rope_lat if latent else rope_ctx
        in_maps.append(m)
    if _cores is not None:
        res = run_bass_kernel_spmd(nc, [in_maps[i] for i in _cores], core_ids=list(range(len(_cores))), trace=_trace)
        return res
    res = run_bass_kernel_spmd(nc, in_maps, core_ids=list(range(8)))
    R = res.results
    y_sample = np.stack([R[b]["yT"].reshape(D, T).T for b in range(4)], axis=0)
    y_prompt = np.concatenate([R[4 + j]["yT"].reshape(D, T).T.reshape(8, 256, D) for j in range(4)], axis=0)
    oCa = np.concatenate([R[4 + j]["o_Caug"] for j in range(4)], axis=0)
    new_C = np.ascontiguousarray(oCa[..., :256])
    new_n = np.ascontiguousarray(oCa[..., 256])
    new_S = np.concatenate([R[4 + j]["o_S"] for j in range(4)], axis=0)
    new_m = np.concatenate([np.transpose(R[4 + j]["o_m"], (3, 1, 2, 0)) for j in range(4)], axis=0)
    return (y_prompt.astype(np.float32), y_sample.astype(np.float32), new_C.astype(np.float32),
            new_n.astype(np.float32), np.ascontiguousarray(new_m).astype(np.float32), new_S.astype(np.float32))
```
